# Optimizing a Trainium2 kernel written in Bass

```python
import math
import jax, jax.numpy as jnp
from jax import lax
import numpy as np

D_MODEL = 1024
BATCH = 8
SEQ = 4096
DEPTH = 1

CHUNK = 64
Q_BLOCK = 128
ATTN_HEADS = 4
ATTN_QK_DIM = 64
ATTN_V_DIM = 2 * ATTN_QK_DIM
ATTN_WIDTH = ATTN_HEADS * ATTN_V_DIM
ATTN_QK_WIDTH = ATTN_HEADS * 2 * ATTN_QK_DIM
SSM_GROUP = 16
SSM_STATE = 64
SSM_WIDTH = D_MODEL // 2
SSM_GROUPS = SSM_WIDTH // SSM_GROUP
N_BRANCH = 2
D_FF = 4 * D_MODEL
IN_COLS = 2 * ATTN_QK_WIDTH + ATTN_WIDTH + SSM_WIDTH + N_BRANCH * D_MODEL
EPS = 1e-6
DT_MIN = 1e-3
DT_MAX = 1e-1

kernel_name = "gated_diffattn_s5_hybrid_block"


def rms_norm(x, g):
    xf = x.astype(jnp.float32)
    y = xf * lax.rsqrt(jnp.mean(xf * xf, axis=-1, keepdims=True) + EPS)
    return (y * g.astype(jnp.float32)).astype(x.dtype)


def lambda_init_fn(layer_idx):
    return 0.8 - 0.6 * math.exp(-0.3 * layer_idx)


def diff_attention(q, k, v, q_norm_g, k_norm_g, lambda_q1, lambda_k1, lambda_q2, lambda_k2,
                   subln_g, layer_idx):
    bsz, seq, _ = q.shape
    dtype = q.dtype
    q = rms_norm(q.reshape(bsz, seq, ATTN_HEADS, 2, ATTN_QK_DIM), q_norm_g)
    k = rms_norm(k.reshape(bsz, seq, ATTN_HEADS, 2, ATTN_QK_DIM), k_norm_g)
    q1 = jnp.transpose(q[:, :, :, 0], (0, 2, 1, 3))
    q2 = jnp.transpose(q[:, :, :, 1], (0, 2, 1, 3))
    k1 = jnp.transpose(k[:, :, :, 0], (0, 2, 1, 3))
    k2 = jnp.transpose(k[:, :, :, 1], (0, 2, 1, 3))
    vh = jnp.transpose(v.reshape(bsz, seq, ATTN_HEADS, ATTN_V_DIM), (0, 2, 1, 3))

    lam_init = lambda_init_fn(layer_idx)
    lam = (jnp.exp(jnp.sum(lambda_q1.astype(jnp.float32) * lambda_k1.astype(jnp.float32)))
           - jnp.exp(jnp.sum(lambda_q2.astype(jnp.float32) * lambda_k2.astype(jnp.float32)))
           + lam_init)
    scale = ATTN_QK_DIM ** -0.5
    k_chunk = jnp.arange(seq) // CHUNK
    n_blocks = seq // Q_BLOCK

    def one_block(i):
        start = i * Q_BLOCK
        q1b = lax.dynamic_slice_in_dim(q1, start, Q_BLOCK, axis=2)
        q2b = lax.dynamic_slice_in_dim(q2, start, Q_BLOCK, axis=2)
        q_chunk = (start + jnp.arange(Q_BLOCK)) // CHUNK
        allowed = k_chunk[None, :] <= q_chunk[:, None]
        s1 = jnp.einsum('bhqd,bhkd->bhqk', q1b, k1).astype(jnp.float32) * scale
        s2 = jnp.einsum('bhqd,bhkd->bhqk', q2b, k2).astype(jnp.float32) * scale
        p1 = jax.nn.softmax(jnp.where(allowed, s1, -jnp.inf), axis=-1)
        p2 = jax.nn.softmax(jnp.where(allowed, s2, -jnp.inf), axis=-1)
        w = (p1 - lam * p2).astype(dtype)
        return jnp.einsum('bhqk,bhkd->bhqd', w, vh)

    o = lax.map(one_block, jnp.arange(n_blocks))
    o = jnp.transpose(o, (1, 0, 3, 2, 4)).reshape(bsz, seq, ATTN_HEADS, ATTN_V_DIM)
    o = rms_norm(o, subln_g) * (1.0 - lam_init)
    return o.reshape(bsz, seq, ATTN_WIDTH).astype(dtype)


def _complex_affine_combine(e1, e2):
    a1r, a1i, b1r, b1i = e1
    a2r, a2i, b2r, b2i = e2
    ar = a1r * a2r - a1i * a2i
    ai = a1r * a2i + a1i * a2r
    br = a2r * b1r - a2i * b1i + b2r
    bi = a2r * b1i + a2i * b1r + b2i
    return (ar, ai, br, bi)


def s5_branch(u, a_re, a_im, log_dt, b_re, b_im, c_re, c_im, d, w_glu, b_glu):
    dtype = u.dtype
    bsz, seq, _ = u.shape
    uf = u.astype(jnp.float32)
    ug = uf.reshape(bsz, seq, SSM_GROUPS, SSM_GROUP)
    ar = a_re.astype(jnp.float32)
    ai = a_im.astype(jnp.float32)
    dt = jnp.exp(log_dt.astype(jnp.float32))[:, None]
    mag = jnp.exp(ar * dt)
    ang = ai * dt
    lb_re = mag * jnp.cos(ang)
    lb_im = mag * jnp.sin(ang)
    den = ar * ar + ai * ai
    nr = lb_re - 1.0
    ni = lb_im
    f_re = (nr * ar + ni * ai) / den
    f_im = (ni * ar - nr * ai) / den
    br = b_re.astype(jnp.float32)
    bi = b_im.astype(jnp.float32)
    bb_re = f_re[..., None] * br - f_im[..., None] * bi
    bb_im = f_re[..., None] * bi + f_im[..., None] * br
    bu_re = jnp.einsum('blgn,gpn->blgp', ug, bb_re)
    bu_im = jnp.einsum('blgn,gpn->blgp', ug, bb_im)
    a_r = jnp.broadcast_to(lb_re, (1, seq, SSM_GROUPS, SSM_STATE))
    a_i = jnp.broadcast_to(lb_im, (1, seq, SSM_GROUPS, SSM_STATE))
    _, _, xr, xi = lax.associative_scan(_complex_affine_combine, (a_r, a_i, bu_re, bu_im), axis=1)
    y = (jnp.einsum('blgp,gnp->blgn', xr, c_re.astype(jnp.float32))
         - jnp.einsum('blgp,gnp->blgn', xi, c_im.astype(jnp.float32)))
    y = y.reshape(bsz, seq, SSM_WIDTH) + d.astype(jnp.float32) * uf
    z = jax.nn.gelu(y).astype(dtype)
    return z * jax.nn.sigmoid(z @ w_glu + b_glu)


def hybrid_layer(x, layer_idx, norm_mix_g, w_in, b_gate, q_norm_g, k_norm_g,
                 lambda_q1, lambda_k1, lambda_q2, lambda_k2, subln_g,
                 ssm_a_re, ssm_a_im, ssm_log_dt, ssm_b_re, ssm_b_im, ssm_c_re, ssm_c_im,
                 ssm_d, w_glu, b_glu, w_proj_attn, w_proj_ssm, w_out,
                 norm_mlp_g, w_mlp_in, w_mlp_out):
    h = rms_norm(x, norm_mix_g)
    proj = h @ w_in
    o1 = ATTN_QK_WIDTH
    o2 = o1 + ATTN_QK_WIDTH
    o3 = o2 + ATTN_WIDTH
    o4 = o3 + SSM_WIDTH
    q, k, v, u, g = proj[..., :o1], proj[..., o1:o2], proj[..., o2:o3], proj[..., o3:o4], proj[..., o4:]
    gates = jax.nn.sigmoid(g + b_gate)
    g_attn, g_ssm = gates[..., :D_MODEL], gates[..., D_MODEL:]

    a = diff_attention(q, k, v, q_norm_g, k_norm_g, lambda_q1, lambda_k1, lambda_q2, lambda_k2,
                       subln_g, layer_idx)
    s = s5_branch(u, ssm_a_re, ssm_a_im, ssm_log_dt, ssm_b_re, ssm_b_im, ssm_c_re, ssm_c_im,
                  ssm_d, w_glu, b_glu)
    merged = g_attn * (a @ w_proj_attn) + g_ssm * (s @ w_proj_ssm)
    x = x + merged @ w_out

    hm = rms_norm(x, norm_mlp_g)
    x = x + jnp.square(jax.nn.relu(hm @ w_mlp_in)) @ w_mlp_out
    return x


def setup_inputs(seed: int = 0) -> dict:
    key = jax.random.key(seed)
    ks = jax.random.split(key, 32)
    L = DEPTH
    f32 = jnp.float32

    def nrm(k, shape, scale):
        return jax.random.normal(k, shape, f32) * scale

    x = jax.random.normal(ks[0], (BATCH, SEQ, D_MODEL), f32)
    a_re = -0.5 + 0.01 * jax.random.normal(ks[1], (L, SSM_GROUPS, SSM_STATE), f32)
    a_im = (jnp.pi * jnp.arange(SSM_STATE, dtype=f32))[None, None, :] \
        + 0.01 * jax.random.normal(ks[2], (L, SSM_GROUPS, SSM_STATE), f32)
    log_dt = jax.random.uniform(ks[3], (L, SSM_GROUPS), f32, math.log(DT_MIN), math.log(DT_MAX))
    return {
        "x": x,
        "norm_mix_g": 1.0 + nrm(ks[4], (L, D_MODEL), 0.02),
        "w_in": nrm(ks[5], (L, D_MODEL, IN_COLS), D_MODEL ** -0.5),
        "b_gate": nrm(ks[6], (L, N_BRANCH * D_MODEL), 0.02),
        "q_norm_g": 1.0 + nrm(ks[7], (L, ATTN_QK_DIM), 0.02),
        "k_norm_g": 1.0 + nrm(ks[8], (L, ATTN_QK_DIM), 0.02),
        "lambda_q1": nrm(ks[9], (L, ATTN_QK_DIM), 0.1),
        "lambda_k1": nrm(ks[10], (L, ATTN_QK_DIM), 0.1),
        "lambda_q2": nrm(ks[11], (L, ATTN_QK_DIM), 0.1),
        "lambda_k2": nrm(ks[12], (L, ATTN_QK_DIM), 0.1),
        "subln_g": 1.0 + nrm(ks[13], (L, ATTN_V_DIM), 0.02),
        "ssm_a_re": a_re,
        "ssm_a_im": a_im,
        "ssm_log_dt": log_dt,
        "ssm_b_re": nrm(ks[14], (L, SSM_GROUPS, SSM_STATE, SSM_GROUP), (0.5 / SSM_GROUP) ** 0.5),
        "ssm_b_im": nrm(ks[15], (L, SSM_GROUPS, SSM_STATE, SSM_GROUP), (0.5 / SSM_GROUP) ** 0.5),
        "ssm_c_re": nrm(ks[16], (L, SSM_GROUPS, SSM_GROUP, SSM_STATE), (0.5 / SSM_STATE) ** 0.5),
        "ssm_c_im": nrm(ks[17], (L, SSM_GROUPS, SSM_GROUP, SSM_STATE), (0.5 / SSM_STATE) ** 0.5),
        "ssm_d": nrm(ks[18], (L, SSM_WIDTH), 1.0),
        "w_glu": nrm(ks[19], (L, SSM_WIDTH, SSM_WIDTH), SSM_WIDTH ** -0.5),
        "b_glu": nrm(ks[20], (L, SSM_WIDTH), 0.02),
        "w_proj_attn": nrm(ks[21], (L, ATTN_WIDTH, D_MODEL), ATTN_WIDTH ** -0.5),
        "w_proj_ssm": nrm(ks[22], (L, SSM_WIDTH, D_MODEL), SSM_WIDTH ** -0.5),
        "w_out": nrm(ks[23], (L, D_MODEL, D_MODEL), D_MODEL ** -0.5),
        "norm_mlp_g": 1.0 + nrm(ks[24], (L, D_MODEL), 0.02),
        "w_mlp_in": nrm(ks[25], (L, D_MODEL, D_FF), D_MODEL ** -0.5),
        "w_mlp_out": nrm(ks[26], (L, D_FF, D_MODEL), D_FF ** -0.5),
    }


def reference(x, norm_mix_g, w_in, b_gate, q_norm_g, k_norm_g, lambda_q1, lambda_k1,
              lambda_q2, lambda_k2, subln_g, ssm_a_re, ssm_a_im, ssm_log_dt, ssm_b_re,
              ssm_b_im, ssm_c_re, ssm_c_im, ssm_d, w_glu, b_glu, w_proj_attn, w_proj_ssm,
              w_out, norm_mlp_g, w_mlp_in, w_mlp_out):
    for l in range(DEPTH):
        x = hybrid_layer(
            x, l, norm_mix_g[l], w_in[l], b_gate[l], q_norm_g[l], k_norm_g[l],
            lambda_q1[l], lambda_k1[l], lambda_q2[l], lambda_k2[l], subln_g[l],
            ssm_a_re[l], ssm_a_im[l], ssm_log_dt[l], ssm_b_re[l], ssm_b_im[l],
            ssm_c_re[l], ssm_c_im[l], ssm_d[l], w_glu[l], b_glu[l],
            w_proj_attn[l], w_proj_ssm[l], w_out[l],
            norm_mlp_g[l], w_mlp_in[l], w_mlp_out[l])
    return x
```

```python
import math
import numpy as np
import ml_dtypes
import concourse.bass as bass
import concourse.mybir as mybir
from concourse.bass_utils import run_bass_kernel_spmd

F32 = mybir.dt.float32
BF16 = mybir.dt.bfloat16
I32 = mybir.dt.int32
AF = mybir.ActivationFunctionType
ALU = mybir.AluOpType
AX = mybir.AxisListType

D = 1024
NCORES = 8
TT = 512
EPS = 1e-6
TWO_PI = 2.0 * math.pi
TWO_PI_S = 6.2831845
GELU_C = 2.0 * math.sqrt(2.0 / math.pi)
LAM_INIT = 0.8 - 0.6 * math.exp(0.0)

PE, ACT, DVE, POOL, SP = "pe", "act", "dve", "pool", "sp"


class Buf:
    __slots__ = ("w", "r", "name")

    def __init__(self, name=""):
        self.w = None
        self.r = {}
        self.name = name


class Prog:
    NDMA = 40

    def __init__(self, nc):
        self.nc = nc
        self.ops = {k: [] for k in (PE, ACT, DVE, POOL, SP)}
        self.cnt = {k: 0 for k in (PE, ACT, DVE, POOL)}
        self.waited = {k: {} for k in (PE, ACT, DVE, POOL, SP)}
        self.dcnt = [0] * self.NDMA
        self.drr = 0
        self.sems = {}
        self.n_instr = 0

    def alloc_sems(self, es):
        for k in (PE, ACT, DVE, POOL):
            self.sems[k] = es.enter_context(self.nc.semaphore("s_" + k))
        for j in range(self.NDMA):
            self.sems[("d", j)] = es.enter_context(self.nc.semaphore("s_d%d" % j))

    def _deps(self, reads, writes):
        deps = {}

        def add(t):
            if t is not None and deps.get(t[0], 0) < t[1]:
                deps[t[0]] = t[1]
        for b in reads:
            add(b.w)
        for b in writes:
            add(b.w)
            for s, v in b.r.items():
                add((s, v))
        return deps

    def _waits(self, eng, deps):
        out = []
        wd = self.waited[eng]
        for s, v in deps.items():
            if s == PE and eng == PE:
                continue
            if s == PE and v > self.cnt[PE]:
                raise RuntimeError("dependency on un-signalled PE milestone (%s)" % eng)
            if wd.get(s, 0) >= v:
                continue
            wd[s] = v
            out.append((s, v))
        return out

    def _mark(self, tok, reads, writes):
        for b in reads:
            if b.r.get(tok[0], 0) < tok[1]:
                b.r[tok[0]] = tok[1]
        for b in writes:
            b.w = tok
            b.r = {}

    def op(self, eng, fn, reads=(), writes=(), ms=True):
        waits = self._waits(eng, self._deps(reads, writes))
        if ms:
            self.cnt[eng] += 1
            tok = (eng, self.cnt[eng])
            inc = (eng, 1)
        else:
            assert eng == PE
            tok = (eng, self.cnt[eng] + 1)
            inc = None
        self.ops[eng].append((waits, fn, inc))
        self._mark(tok, reads, writes)
        self.n_instr += 1
        return tok

    def dma(self, out, in_, reads=(), writes=(), q=SP, slow=False):
        j = self.drr
        self.drr = (j + 1) % self.NDMA
        deps = self._deps(reads, writes)
        key = ("d", j)
        if self.dcnt[j] > 0:
            v = 16 * self.dcnt[j]
            if deps.get(key, 0) < v:
                deps[key] = v
        waits = self._waits(q, deps)
        self.dcnt[j] += 1
        tok = (key, 16 * self.dcnt[j])
        if slow:
            fn = lambda e: e.dma_start(out=out, in_=in_, allow_slow_non_contiguous=True)
        else:
            fn = lambda e: e.dma_start(out=out, in_=in_)
        self.ops[q].append((waits, fn, (key, 16)))
        self._mark(tok, reads, writes)
        self.n_instr += 1
        return tok

    def barrier(self):
        deps = {k: self.cnt[k] for k in (PE, ACT, DVE, POOL) if self.cnt[k] > 0}
        for j in range(self.NDMA):
            if self.dcnt[j] > 0:
                deps[("d", j)] = 16 * self.dcnt[j]
        for eng in (PE, ACT, DVE, POOL, SP):
            waits = []
            wd = self.waited[eng]
            for s_, v in deps.items():
                if wd.get(s_, 0) >= v:
                    continue
                wd[s_] = v
                waits.append((s_, v))
            if waits:
                self.ops[eng].append((waits, None, None))

    def wait_tok(self, eng, toks):
        deps = {}
        for t in toks:
            if deps.get(t[0], 0) < t[1]:
                deps[t[0]] = t[1]
        waits = self._waits(eng, deps)
        if waits:
            self.ops[eng].append((waits, None, None))

    def emit(self):
        nc = self.nc
        sems = self.sems

        def run(e, lst):
            for waits, fn, inc in lst:
                for s, v in waits:
                    e.wait_ge(sems[s], v)
                if fn is not None:
                    ins = fn(e)
                    if inc is not None:
                        ins.then_inc(sems[inc[0]], inc[1])
        with nc.Block() as block:
            @block.sync
            def _(e):
                run(e, self.ops[SP])

            @block.scalar
            def _(e):
                run(e, self.ops[ACT])

            @block.tensor
            def _(e):
                run(e, self.ops[PE])

            @block.vector
            def _(e):
                run(e, self.ops[DVE])

            @block.gpsimd
            def _(e):
                run(e, self.ops[POOL])


class SbAlloc:
    def __init__(self, nc, limit):
        self.nc = nc
        self.off = 16512
        self.limit = limit
        self.n = 0

    def __call__(self, shape, dt, name=None, at=None):
        esz = 2 if dt == BF16 else 4
        nbytes = esz
        for s in shape[1:]:
            nbytes *= s
        nbytes = (nbytes + 63) // 64 * 64
        if at is None:
            at = self.off
            self.off += nbytes
            assert self.off <= self.limit, "SBUF overflow %d" % self.off
        self.n += 1
        return self.nc.alloc_sbuf_tensor_at(name or ("t%d" % self.n), list(shape), dt, offset=at)


def build(S=4096, debug=None):
    NT = S // TT
    NKT = S // 128
    nc = bass.Bass("TRN2", target_bir_lowering=False)
    P = Prog(nc)

    def din(name, shape, dt=F32):
        return nc.dram_tensor(name, list(shape), dt, kind="ExternalInput").ap()

    x_d = din("x", [S, D])
    w_in_d = din("w_in", [D, 4096])
    w_mi_d = din("w_mlp_in", [D, 4096])
    w_mo_d = din("w_mlp_out", [4096, D])
    w_out_d = din("w_out", [D, D])
    w_pa_d = din("w_proj_attn", [512, D])
    w_ps_d = din("w_proj_ssm", [512, D])
    w_glu_d = din("w_glu", [512, 512])
    g_mix_d = din("norm_mix_g", [D])
    g_mlp_d = din("norm_mlp_g", [D])
    b_gate_d = din("b_gate", [2048])
    qg_d = din("q_norm_g", [64])
    kg_d = din("k_norm_g", [64])
    lq1_d = din("lambda_q1", [64])
    lk1_d = din("lambda_k1", [64])
    lq2_d = din("lambda_q2", [64])
    lk2_d = din("lambda_k2", [64])
    subg_d = din("subln_g", [128])
    are_d = din("ssm_a_re", [32, 64])
    aim_d = din("ssm_a_im", [32, 64])
    ldt_d = din("ssm_log_dt", [32])
    bre_d = din("ssm_b_re", [32, 64, 16])
    bim_d = din("ssm_b_im", [32, 64, 16])
    cre_d = din("ssm_c_re", [32, 16, 64])
    cim_d = din("ssm_c_im", [32, 16, 64])
    dd_d = din("ssm_d", [512])
    bglu_d = din("b_glu", [512])
    ident_d = din("c_ident", [128, 128], BF16)
    ones_d = din("c_ones", [128, 128], BF16)
    blk_d = din("c_blk", [128, 128], BF16)
    iota_d = din("c_iota", [128, 512])
    rmask_d = din("c_rmask", [128, 8])
    hsel_d = din("c_hsel", [128, 8])
    out_d = nc.dram_tensor("out", [S, D], F32, kind="ExternalOutput").ap()

    def dscr(name, shape):
        return nc.dram_tensor(name, list(shape), BF16, kind="Internal").ap()

    wb_in = dscr("wb_in", [D, 4096])
    wb_mi = dscr("wb_mi", [D, 4096])
    wb_mo = dscr("wb_mo", [4096, D])
    wb_out = dscr("wb_out", [D, D])
    wb_pa = dscr("wb_pa", [512, D])
    wb_ps = dscr("wb_ps", [512, D])
    wb_glu = dscr("wb_glu", [512, 512])

    dbg_outs = {}

    def dump(name, src_ap, shape, dt, reads):
        if debug is None or name not in debug:
            return
        t = nc.dram_tensor("dbg_" + name, list(shape), dt, kind="ExternalOutput").ap()
        dbg_outs[name] = P.dma(t, src_ap, reads=reads, writes=[Buf()])

    from contextlib import ExitStack
    with ExitStack() as es:
        P.alloc_sems(es)
        sb = SbAlloc(nc, 229376)
        psum = [es.enter_context(nc.psum_tensor("ps%d" % i, [128, 512], F32)) for i in range(8)]
        psb = [Buf("ps%d" % i) for i in range(8)]

        ident = sb([128, 128], BF16, "ident")
        ones = sb([128, 128], BF16, "ones")
        blk = sb([128, 128], BF16, "blk")
        iota = sb([128, 512], F32, "iota")
        rmask = sb([128, 8], F32, "rmask")
        hsel = sb([128, 8], F32, "hsel")
        cB = Buf("consts")
        P.dma(ident[:], ident_d, writes=[cB])
        P.dma(ones[:], ones_d, writes=[cB])
        P.dma(blk[:], blk_d, writes=[cB])
        P.dma(iota[:], iota_d, writes=[cB])
        P.dma(rmask[:], rmask_d, writes=[cB])
        P.dma(hsel[:], hsel_d, writes=[cB])
        negpi = hsel[:, 3:4]
        zeros = sb([128, 128], BF16, "zeros")
        P.op(POOL, lambda e: e.memset(zeros[:], 0.0), writes=[cB])
        epsc = hsel[:, 4:5]
        hpi = hsel[:, 5:6]
        halfc = hsel[:, 6:7]
        quartc = hsel[:, 7:8]

        gmix = sb([128, 8], F32, "gmix")
        gmlp = sb([128, 8], F32, "gmlp")
        gsub = sb([128, 1], F32, "gsub")
        bgate = sb([128, 16], F32, "bgate")
        bglu = sb([128, 4], F32, "bglu")
        dcol = sb([128, 4], F32, "dcol")
        qg = sb([128, 1], F32, "qg")
        kg = sb([128, 1], F32, "kg")
        lamt = sb([128, 4, 64], F32, "lamt", at=150 * 1024)
        neglam = sb([128, 1], F32, "neglam")
        sB = Buf("smallparams")
        P.dma(gmix[:], g_mix_d.rearrange("(k p) -> p k", p=128), writes=[sB], slow=True)
        P.dma(gmlp[:], g_mlp_d.rearrange("(k p) -> p k", p=128), writes=[sB], slow=True)
        P.dma(gsub[:], subg_d.rearrange("(p o) -> p o", o=1), writes=[sB], slow=True)
        P.dma(bgate[:], b_gate_d.rearrange("(k p) -> p k", p=128), writes=[sB], slow=True)
        P.dma(bglu[:], bglu_d.rearrange("(k p) -> p k", p=128), writes=[sB], slow=True)
        P.dma(dcol[:], dd_d.rearrange("(k p) -> p k", p=128), writes=[sB], slow=True)
        for h in range(2):
            P.dma(qg[64 * h:64 * h + 64, :], qg_d.rearrange("(p o) -> p o", o=1), writes=[sB], slow=True)
            P.dma(kg[64 * h:64 * h + 64, :], kg_d.rearrange("(p o) -> p o", o=1), writes=[sB], slow=True)
        for i, t in enumerate((lq1_d, lk1_d, lq2_d, lk2_d)):
            P.dma(lamt[:, i, :], t.partition_broadcast(128), writes=[sB])
        lsum = sb([128, 2], F32, "lsum", at=150 * 1024 + 1024)
        lprod = sb([128, 2, 64], F32, "lprod", at=150 * 1024 + 1024 + 64)
        P.op(DVE, lambda e: e.tensor_tensor(out=lprod[:, 0, :], in0=lamt[:, 0, :], in1=lamt[:, 1, :], op=ALU.mult), reads=[sB], writes=[sB])
        P.op(DVE, lambda e: e.tensor_tensor(out=lprod[:, 1, :], in0=lamt[:, 2, :], in1=lamt[:, 3, :], op=ALU.mult), reads=[sB], writes=[sB])
        P.op(DVE, lambda e: e.reduce_sum(out=lsum[:, 0:1], in_=lprod[:, 0, :], axis=AX.X), reads=[sB], writes=[sB])
        P.op(DVE, lambda e: e.reduce_sum(out=lsum[:, 1:2], in_=lprod[:, 1, :], axis=AX.X), reads=[sB], writes=[sB])
        P.op(ACT, lambda e: e.activation(out=lsum[:], in_=lsum[:], func=AF.Exp), reads=[sB], writes=[sB])
        P.op(DVE, lambda e: e.tensor_tensor(out=neglam[:], in0=lsum[:, 1:2], in1=lsum[:, 0:1], op=ALU.subtract), reads=[sB], writes=[sB])
        P.op(DVE, lambda e: e.tensor_scalar(out=neglam[:], in0=neglam[:], scalar1=-LAM_INIT, scalar2=None, op0=ALU.add), reads=[sB], writes=[sB])
        P.op(DVE, lambda e: e.tensor_scalar(out=bgate[:], in0=bgate[:], scalar1=0.5, scalar2=None, op0=ALU.mult), reads=[sB], writes=[sB])
        P.op(DVE, lambda e: e.tensor_scalar(out=bglu[:], in0=bglu[:], scalar1=0.5, scalar2=None, op0=ALU.mult), reads=[sB], writes=[sB])
        P.op(DVE, lambda e: e.tensor_scalar(out=gsub[:], in0=gsub[:], scalar1=1.0 - LAM_INIT, scalar2=None, op0=ALU.mult), reads=[sB], writes=[sB])

        rho = sb([128, 32], F32, "rho")
        thr = sb([128, 32], F32, "thr")
        toff = sb([128, 32], F32, "toff")
        ki32 = sb([128, 32], I32, "ki32")
        BBt = sb([128, 4, 2, 128], BF16, "BBt")
        CC = sb([128, 32, 2, 64], BF16, "CC")
        wglu = sb([128, 4, 512], BF16, "wglu")
        carry = sb([128, 32], F32, "carry")

        prep_mark = sb.off
        NST = 5
        stf = [sb([128, 2048], F32, "stf%d" % i) for i in range(NST)]
        stb = [sb([128, 2048], BF16, "stb%d" % i) for i in range(NST)]
        stfB = [Buf() for _ in range(NST)]
        stbB = [Buf() for _ in range(NST)]
        wscrB = Buf("wscratch")
        pi = [0]

        def prep(src, dst, R, C, scale):
            cw = min(C, 2048)
            out = []
            for kc in range(R // 128):
                for c0 in range(0, C, cw):
                    out.append((5.0, lambda kc=kc, c0=c0: prep_chunk(src, dst, cw, scale, kc, c0)))
            return out

        def prep_chunk(src, dst, cw, scale, kc, c0):
            if True:
                if True:
                    i = pi[0] % NST
                    pi[0] += 1
                    P.dma(stf[i][:, 0:cw], src[kc * 128:(kc + 1) * 128, c0:c0 + cw], writes=[stfB[i]])
                    sc = scale(kc) if scale is not None else None
                    if pi[0] % 2 == 0:
                        qeng = ACT
                        if sc is None:
                            P.op(ACT, lambda e, i=i: e.activation(out=stb[i][:, 0:cw], in_=stf[i][:, 0:cw], func=AF.Copy),
                                 reads=[stfB[i]], writes=[stbB[i]])
                        else:
                            P.op(ACT, lambda e, i=i, sc=sc: e.activation(out=stb[i][:, 0:cw], in_=stf[i][:, 0:cw], func=AF.Copy, scale=sc),
                                 reads=[stfB[i], sB, cB], writes=[stbB[i]])
                    else:
                        qeng = ACT
                        if sc is None:
                            P.op(DVE, lambda e, i=i: e.tensor_copy(out=stb[i][:, 0:cw], in_=stf[i][:, 0:cw]),
                                 reads=[stfB[i]], writes=[stbB[i]])
                        else:
                            P.op(DVE, lambda e, i=i, sc=sc: e.tensor_scalar(out=stb[i][:, 0:cw], in0=stf[i][:, 0:cw], scalar1=sc, scalar2=None, op0=ALU.mult),
                                 reads=[stfB[i], sB, cB], writes=[stbB[i]])
                    P.dma(dst[kc * 128:(kc + 1) * 128, c0:c0 + cw], stb[i][:, 0:cw], reads=[stbB[i]], writes=[wscrB], q=qeng)

        for c_, f_ in prep(w_in_d, wb_in, D, 4096, lambda kc: gmix[:, kc:kc + 1]) + prep(w_glu_d, wb_glu, 512, 512, lambda kc: halfc):
            f_()
        late_prep = (prep(w_pa_d, wb_pa, 512, D, lambda kc: gsub[:, 0:1]) + prep(w_ps_d, wb_ps, 512, D, lambda kc: quartc)
                     + prep(w_out_d, wb_out, D, D, None) + prep(w_mi_d, wb_mi, D, 4096, lambda kc: gmlp[:, kc:kc + 1])
                     + prep(w_mo_d, wb_mo, 4096, D, None))
        prep_end = sb.off

        sb.off = prep_end
        tB = Buf("ssm_tables")
        ar = sb([128, 32], F32)
        ai = sb([128, 32], F32)
        dt = sb([128, 32], F32)
        for h in range(2):
            P.dma(ar[64 * h:64 * h + 64, :], are_d.rearrange("g p -> p g"), writes=[tB], slow=True)
            P.dma(ai[64 * h:64 * h + 64, :], aim_d.rearrange("g p -> p g"), writes=[tB], slow=True)
        P.dma(dt[:], ldt_d.partition_broadcast(128), writes=[tB])
        t0 = sb([128, 32], F32)
        t1 = sb([128, 32], F32)
        t2 = sb([128, 32], F32)
        t3 = sb([128, 32], F32)
        fre = sb([128, 32], F32)
        fim = sb([128, 32], F32)

        def dv(fn):
            P.op(DVE, fn, reads=[tB, cB], writes=[tB])

        def ac(fn):
            P.op(ACT, fn, reads=[tB, cB], writes=[tB])

        TT_ = lambda e, o, a, b, op: e.tensor_tensor(out=o, in0=a, in1=b, op=op)
        ac(lambda e: e.activation(out=dt[:], in_=dt[:], func=AF.Exp))
        dv(lambda e: TT_(e, t0[:], ar[:], dt[:], ALU.mult))
        ac(lambda e: e.activation(out=rho[:], in_=t0[:], func=AF.Exp))
        dv(lambda e: TT_(e, t1[:], ai[:], dt[:], ALU.mult))
        dv(lambda e: e.tensor_scalar(out=t1[:], in0=t1[:], scalar1=1.0 / TWO_PI, scalar2=None, op0=ALU.mult))
        dv(lambda e: e.tensor_copy(out=ki32[:], in_=t1[:]))
        dv(lambda e: TT_(e, thr[:], t1[:], ki32[:], ALU.subtract))
        ac(lambda e: e.activation(out=t2[:], in_=thr[:], func=AF.Abs))
        ac(lambda e: e.activation(out=t1[:], in_=thr[:], func=AF.Sin, scale=TWO_PI_S))
        ac(lambda e: e.activation(out=t2[:], in_=t2[:], func=AF.Sin, scale=-TWO_PI_S, bias=hpi))
        dv(lambda e: TT_(e, t3[:], t2[:], rho[:], ALU.mult))
        dv(lambda e: TT_(e, t1[:], t1[:], rho[:], ALU.mult))
        dv(lambda e: e.tensor_scalar(out=t3[:], in0=t3[:], scalar1=-1.0, scalar2=None, op0=ALU.add))
        dv(lambda e: TT_(e, t0[:], ar[:], ar[:], ALU.mult))
        dv(lambda e: TT_(e, t2[:], ai[:], ai[:], ALU.mult))
        dv(lambda e: TT_(e, t0[:], t0[:], t2[:], ALU.add))
        dv(lambda e: e.reciprocal(out=t0[:], in_=t0[:]))
        dv(lambda e: TT_(e, fre[:], t3[:], ar[:], ALU.mult))
        dv(lambda e: TT_(e, t2[:], t1[:], ai[:], ALU.mult))
        dv(lambda e: TT_(e, fre[:], fre[:], t2[:], ALU.add))
        dv(lambda e: TT_(e, fre[:], fre[:], t0[:], ALU.mult))
        dv(lambda e: TT_(e, fim[:], t1[:], ar[:], ALU.mult))
        dv(lambda e: TT_(e, t2[:], t3[:], ai[:], ALU.mult))
        dv(lambda e: TT_(e, fim[:], fim[:], t2[:], ALU.subtract))
        dv(lambda e: TT_(e, fim[:], fim[:], t0[:], ALU.mult))
        bnr = sb([128, 32, 16], F32)
        bni = sb([128, 32, 16], F32)
        for h in range(2):
            P.dma(bnr[64 * h:64 * h + 64, :, :], bre_d.rearrange("g p m -> p g m"), writes=[tB], slow=True)
            P.dma(bni[64 * h:64 * h + 64, :, :], bim_d.rearrange("g p m -> p g m"), writes=[tB], slow=True)
        bbr = sb([128, 32, 16], F32)
        bbi = sb([128, 32, 16], F32)
        tmpb = sb([128, 32, 16], F32)
        freb = fre[:].unsqueeze(2).to_broadcast([128, 32, 16])
        fimb = fim[:].unsqueeze(2).to_broadcast([128, 32, 16])
        dv(lambda e: TT_(e, bbr[:], bnr[:], freb, ALU.mult))
        dv(lambda e: TT_(e, tmpb[:], bni[:], fimb, ALU.mult))
        dv(lambda e: TT_(e, bbr[:], bbr[:], tmpb[:], ALU.subtract))
        dv(lambda e: TT_(e, bbi[:], bni[:], freb, ALU.mult))
        dv(lambda e: TT_(e, tmpb[:], bnr[:], fimb, ALU.mult))
        dv(lambda e: TT_(e, bbi[:], bbi[:], tmpb[:], ALU.add))
        nnat = sb([128, 32, 16], BF16)
        nsw = sb([128, 32, 16], BF16)
        dv(lambda e: e.tensor_scalar(out=tmpb[:], in0=bbr[:], scalar1=hsel[:, 0:1], scalar2=None, op0=ALU.mult))
        dv(lambda e: e.scalar_tensor_tensor(out=nnat[:], in0=bbi[:], scalar=hsel[:, 1:2], in1=tmpb[:], op0=ALU.mult, op1=ALU.add))
        dv(lambda e: e.tensor_scalar(out=tmpb[:], in0=bbi[:], scalar1=hsel[:, 0:1], scalar2=None, op0=ALU.mult))
        dv(lambda e: e.scalar_tensor_tensor(out=bbr[:], in0=bbr[:], scalar=hsel[:, 1:2], in1=tmpb[:], op0=ALU.mult, op1=ALU.subtract))
        dv(lambda e: e.tensor_scalar(out=nsw[:], in0=bbr[:], scalar1=-1.0, scalar2=None, op0=ALU.mult))
        bbB = Buf("BB")
        for j in range(4):
            for v, src in enumerate((nnat, nsw)):
                pt = psum[(2 * j + v) % 8]
                ptb = psb[(2 * j + v) % 8]
                pv = pt[:].bitcast(BF16)[:, 0:128]
                P.op(PE, lambda e, pv=pv, src=src, j=j: e.transpose(out=pv, in_=src[:, 8 * j:8 * j + 8, :].rearrange("p g m -> p (g m)"), identity=ident[:]),
                     reads=[tB, cB], writes=[ptb])
                P.op(DVE, lambda e, pv=pv, j=j, v=v: e.tensor_copy(out=BBt[:, j, v, :], in_=pv), reads=[ptb], writes=[bbB])
        cst = sb([128, 4, 2, 64], F32)
        cstb = sb([128, 4, 2, 2, 64], BF16)
        P.dma(cst[:, :, 0, :], cre_d.rearrange("(j g) n p -> (g n) j p", j=4), writes=[tB])
        P.dma(cst[:, :, 1, :], cim_d.rearrange("(j g) n p -> (g n) j p", j=4), writes=[tB])
        dv(lambda e: e.tensor_copy(out=cstb[:, :, 0, 0, :], in_=cst[:, :, 0, :]))
        dv(lambda e: e.tensor_scalar(out=cstb[:, :, 0, 1, :], in0=cst[:, :, 1, :], scalar1=-1.0, scalar2=None, op0=ALU.mult))
        dv(lambda e: e.tensor_scalar(out=cstb[:, :, 1, 0, :], in0=cst[:, :, 1, :], scalar1=-1.0, scalar2=None, op0=ALU.mult))
        dv(lambda e: e.tensor_scalar(out=cstb[:, :, 1, 1, :], in0=cst[:, :, 0, :], scalar1=-1.0, scalar2=None, op0=ALU.mult))
        ccB = Buf("CC")
        P.op(POOL, lambda e: e.memset(CC[:], 0.0), writes=[ccB])
        for j in range(4):
            for v in range(2):
                pt = psum[(2 * j + v) % 8]
                ptb = psb[(2 * j + v) % 8]
                pv = pt[:].bitcast(BF16)[:, 0:128]
                P.op(PE, lambda e, pv=pv, j=j, v=v: e.transpose(out=pv, in_=cstb[:, j, v, :, :].rearrange("p h q -> p (h q)"), identity=ident[:]),
                     reads=[tB, cB], writes=[ptb])
                for gp in range(8):
                    P.op(DVE, lambda e, pv=pv, j=j, v=v, gp=gp: e.tensor_copy(out=CC[:, 8 * j + gp, v, 16 * (gp % 4):16 * (gp % 4) + 16], in_=pv[:, 16 * gp:16 * gp + 16]),
                         reads=[ptb], writes=[ccB])
        wgB = Buf("wglu")
        P.dma(wglu[:], wb_glu.rearrange("(k p) c -> p k c", p=128), reads=[wscrB], writes=[wgB])
        carB = Buf("carry")
        P.op(POOL, lambda e: e.memset(carry[:], 0.0), writes=[carB])
        temps_end = sb.off
        sb.off = prep_mark
        Kc = sb([128, 4, S], BF16, "Kc")
        Vc = sb([128, NKT, 512], BF16, "Vc")
        KB = [Buf("K%d" % t) for t in range(NT)]
        VB = [Buf("V%d" % t) for t in range(NT)]
        xt = sb([128, 4, D], F32, "xt")
        xB = [Buf("x%d" % s) for s in range(4)]
        sb.off = max(sb.off, temps_end)
        xst = sb([128, D], F32, "xst")
        xstB = Buf("xst")
        hT = [sb([128, 8, TT], BF16, "hT%d" % i) for i in range(2)]
        hB = [Buf("hT0"), Buf("hT1")]
        xn = sb([128, D], BF16, "xn")
        xnB = Buf("xn")
        nstat = sb([128, 8], F32, "nstat")
        nsB = Buf("nstat")
        qT = sb([128, 4, TT], BF16, "qT")
        qB = Buf("qT")
        uT = sb([128, 4, TT], BF16, "uT")
        uB = Buf("uT")
        zT = sb([128, 4, TT], BF16, "zT")
        zB = Buf("zT")
        sT = [sb([128, 4, TT], BF16, "sT%d" % i) for i in range(2)]
        stB_ = [Buf("sT0"), Buf("sT1")]
        aT_ = sb([128, 4, TT], BF16, "aT")
        aB = Buf("aT")
        mrg = sb([128, 16, TT], BF16, "mrg_ff")
        mB = Buf("mrg_ff")
        e_t = [[sb([128, TT], BF16, "e%d%d" % (i, j)) for j in range(2)] for i in range(2)]
        eB = [[Buf() for j in range(2)] for i in range(2)]
        f1 = sb([128, TT], F32, "f1")
        f2 = sb([128, TT], F32, "f2")
        fB = [Buf("f1"), Buf("f2")]
        g1 = sb([128, TT], BF16, "g1")
        g2 = sb([128, TT], BF16, "g2")
        g3 = sb([128, TT], BF16, "g3")
        gB = [Buf("g1"), Buf("g2"), Buf("g3")]
        nS = [sb([128, TT], F32, "nS%d" % i) for i in range(2)]
        nC = [sb([128, TT], F32, "nC%d" % i) for i in range(2)]
        zb = [sb([128, TT], F32, "zb%d" % i) for i in range(2)]
        zt2 = sb([128, TT], F32, "zt2")
        yf2 = zt2
        v12 = [sb([128, 2, TT], BF16, "v12_%d" % i) for i in range(2)]
        um = [sb([128, TT], BF16, "um%d" % i) for i in range(2)]
        nSB = [Buf() for _ in range(2)]
        nCB = [Buf() for _ in range(2)]
        zbB = [Buf() for _ in range(2)]
        zt2B = Buf()
        yf = f1
        yfB = [fB[0], zt2B]
        vB = [Buf() for _ in range(2)]
        umB = [Buf() for _ in range(2)]
        NSL = 2
        wsl = [sb([128, 4096], BF16, "wsl%d" % i) for i in range(NSL)]
        wslB = [Buf("wsl%d" % i) for i in range(NSL)]
        print("SBUF bytes used per partition:", sb.off)

        wstate = {"i": 0}

        def wload(src_ap, view):
            i = wstate["i"] % NSL
            wstate["i"] += 1
            a, b = view
            dst = wsl[i][:].rearrange("p (a b) -> p a b", a=a)
            P.dma(dst, src_ap, reads=[wscrB], writes=[wslB[i]])
            return dst, wslB[i]

        def w_cols(wb, c0):
            return wb[:, c0:c0 + 512].rearrange("(k p) c -> p k c", p=128)

        def w_rows(wb, r0):
            return wb[r0:r0 + 512, :].rearrange("(k p) c -> p k c", p=128)

        psi = [0]

        def nps():
            i = psi[0] % 4
            psi[0] += 1
            return psum[i], psb[i]

        def bank(i):
            return psum[i], psb[i]

        def norm_sub(src_ap, srcB, hdst, hdstB, s, pt, ptb):
            P.op(POOL, lambda e: e.memset(nstat[:, s:s + 1], 0.0), writes=[nsB])
            P.op(ACT, lambda e: e.activation(out=xn[:], in_=src_ap, func=AF.Square, accum_out=nstat[:, s:s + 1]),
                 reads=[srcB], writes=[xnB, nsB])
            P.op(ACT, lambda e: e.activation(out=nstat[:, 4 + s:5 + s], in_=nstat[:, s:s + 1], func=AF.Ln, scale=1.0 / D, bias=epsc), reads=[nsB, cB], writes=[nsB])
            P.op(ACT, lambda e: e.activation(out=nstat[:, 4 + s:5 + s], in_=nstat[:, 4 + s:5 + s], func=AF.Exp, scale=-0.5), reads=[nsB], writes=[nsB])
            P.op(DVE, lambda e: e.tensor_scalar(out=xn[:], in0=src_ap, scalar1=nstat[:, 4 + s:5 + s], scalar2=None, op0=ALU.mult),
                 reads=[srcB, nsB], writes=[xnB])
            pv = pt[:].bitcast(BF16)
            for dk in range(8):
                P.op(PE, lambda e, dk=dk: e.transpose(out=pv[:, dk * 128:(dk + 1) * 128], in_=xn[:, dk * 128:(dk + 1) * 128], identity=ident[:]),
                     reads=[xnB, cB], writes=[ptb], ms=(dk == 7))
            P.op(ACT, lambda e: e.activation(out=hdst[:, :, s * 128:(s + 1) * 128], in_=pv.rearrange("p (k t) -> p k t", k=8), func=AF.Copy),
                 reads=[ptb], writes=[hdstB])

        def proj_fm(wslot, wbuf, ct, rhs_fn, nk, rbufs, pt, ptb):
            for k in range(nk):
                P.op(PE, lambda e, k=k: e.matmul(pt[:], lhsT=wslot[:, k, ct * 128:(ct + 1) * 128], rhs=rhs_fn(k), start=(k == 0), stop=(k == nk - 1)),
                     reads=[wbuf] + rbufs, writes=[ptb], ms=(k == nk - 1))

        def rsqrt_from(out_ap, in_ap, inv_n, rbufs, wbufs):
            P.op(ACT, lambda e: e.activation(out=out_ap, in_=in_ap, func=AF.Ln, scale=inv_n, bias=epsc), reads=rbufs + [cB], writes=wbufs)
            P.op(ACT, lambda e: e.activation(out=out_ap, in_=out_ap, func=AF.Exp, scale=-0.5), reads=wbufs, writes=wbufs)

        def lookahead_items(T):
            tok0 = T * TT
            hb = T % 2
            items = []

            def front(s):
                P.dma(xst[:], x_d[tok0 + s * 128: tok0 + (s + 1) * 128, :], writes=[xstB])
                pt, ptb = bank(4 + s % 3)
                norm_sub(xst[:], xstB, hT[hb], hB[hb], s, pt, ptb)
            for s in range(4):
                items.append((3.0, lambda s=s: front(s)))

            def uproj():
                wslot, wbuf = wload(w_cols(wb_in, 1536), (8, 512))
                for j in range(4):
                    pt, ptb = bank(4 + j % 3)
                    proj_fm(wslot, wbuf, j, lambda k: hT[hb][:, k, :], 8, [hB[hb]], pt, ptb)
                    P.op(ACT, lambda e, pt=pt, j=j: e.activation(out=uT[:, j, :], in_=pt[:], func=AF.Copy), reads=[ptb], writes=[uB])
                if T == 0:
                    dump("hT", hT[0][:], [128, 8, TT], BF16, [hB[0]])
                    dump("uT", uT[:], [128, 4, TT], BF16, [uB])
                P.op(DVE, lambda e: e.tensor_scalar(out=toff[:], in0=thr[:], scalar1=float(tok0), scalar2=None, op0=ALU.mult), reads=[tB], writes=[tB])
                P.op(DVE, lambda e: e.tensor_copy(out=ki32[:], in_=toff[:]), reads=[tB], writes=[tB])
                P.op(DVE, lambda e: e.tensor_tensor(out=toff[:], in0=toff[:], in1=ki32[:], op=ALU.subtract), reads=[tB], writes=[tB])
            items.append((8.0, uproj))

            pn, pnb = bank(4)
            psw, pswb = bank(5)

            def e1(g):
                j, gp, tb = g // 8, g % 8, g % 2
                P.op(ACT, lambda e: e.activation(out=um[tb][:], in_=uT[:, j, :], func=AF.Copy, scale=rmask[:, gp:gp + 1]), reads=[uB, cB], writes=[umB[tb]])
                P.op(ACT, lambda e: e.activation(out=nS[tb][:], in_=iota[:], func=AF.Identity, scale=thr[:, g:g + 1], bias=toff[:, g:g + 1]),
                     reads=[tB, cB], writes=[nSB[tb]])

            def e1b(g):
                tb = g % 2
                nci = nC[tb][:].bitcast(I32)
                P.op(DVE, lambda e: e.tensor_copy(out=nci, in_=nS[tb][:]), reads=[nSB[tb]], writes=[nCB[tb]])
                P.op(DVE, lambda e: e.tensor_copy(out=nC[tb][:], in_=nci), reads=[nCB[tb]], writes=[nCB[tb]])
                P.op(DVE, lambda e: e.tensor_tensor(out=nS[tb][:], in0=nS[tb][:], in1=nC[tb][:], op=ALU.subtract), reads=[nCB[tb], nSB[tb]], writes=[nSB[tb]])
                P.op(ACT, lambda e: e.activation(out=nC[tb][:], in_=nS[tb][:], func=AF.Abs), reads=[nSB[tb]], writes=[nCB[tb]])
                P.op(ACT, lambda e: e.activation(out=nC[tb][:], in_=nC[tb][:], func=AF.Sin, scale=-TWO_PI_S, bias=hpi), reads=[nCB[tb], cB], writes=[nCB[tb]])
                P.op(ACT, lambda e: e.activation(out=nS[tb][:], in_=nS[tb][:], func=AF.Sin, scale=TWO_PI_S), reads=[nSB[tb]], writes=[nSB[tb]])

            def pe_bu(g):
                j, tb = g // 8, g % 2
                P.op(PE, lambda e: e.matmul(pn[:], lhsT=BBt[:, j, 0, :], rhs=um[tb][:], start=True, stop=True), reads=[bbB, umB[tb]], writes=[pnb])
                P.op(PE, lambda e: e.matmul(psw[:], lhsT=BBt[:, j, 1, :], rhs=um[tb][:], start=True, stop=True), reads=[bbB, umB[tb]], writes=[pswb])

            def e2a(g):
                tb = g % 2
                P.op(DVE, lambda e: e.tensor_tensor(out=zb[tb][:], in0=pn[:], in1=nC[tb][:], op=ALU.mult), reads=[pnb, nCB[tb]], writes=[zbB[tb]])
                P.op(DVE, lambda e: e.tensor_tensor(out=zt2[:], in0=psw[:], in1=nS[tb][:], op=ALU.mult), reads=[pswb, nSB[tb]], writes=[zt2B])
                P.op(DVE, lambda e: e.tensor_tensor(out=zb[tb][:], in0=zb[tb][:], in1=zt2[:], op=ALU.add), reads=[zt2B, zbB[tb]], writes=[zbB[tb]])

            def e2b(g):
                tb = g % 2
                P.op(DVE, lambda e: e.tensor_tensor_scan(out=zb[tb][:], data0=rho[:, g:g + 1].to_broadcast([128, TT]), data1=zb[tb][:], initial=carry[:, g:g + 1], op0=ALU.mult, op1=ALU.add),
                     reads=[zbB[tb], tB, carB], writes=[zbB[tb]])
                P.op(DVE, lambda e: e.tensor_copy(out=carry[:, g:g + 1], in_=zb[tb][:, TT - 1:TT]), reads=[zbB[tb]], writes=[carB])
                P.op(DVE, lambda e: e.tensor_tensor(out=v12[tb][:, 0, :], in0=zb[tb][:], in1=nC[tb][:], op=ALU.mult), reads=[zbB[tb], nCB[tb]], writes=[vB[tb]])
                P.op(POOL, lambda e: e.tensor_tensor(out=v12[tb][:, 1, :], in0=zb[tb][:], in1=nS[tb][:], op=ALU.mult), reads=[zbB[tb], nSB[tb]], writes=[vB[tb]])

            def pe_y(g):
                j, gp, tb = g // 8, g % 8, g % 2
                hh = gp // 4
                ypt, yptb = bank(6)
                for v in range(2):
                    P.op(PE, lambda e, v=v: e.matmul(ypt[64 * hh:64 * hh + 64, :], lhsT=CC[:, g, v, :], rhs=v12[tb][:, v, :],
                                                     start=(gp % 4 == 0 and v == 0), stop=(gp % 4 == 3 and v == 1)),
                         reads=[ccB, vB[tb]], writes=[yptb], ms=True)

            def gelu(j):
                ypt, yptb = bank(6)
                P.op(DVE, lambda e: e.scalar_tensor_tensor(out=yf[:], in0=uT[:, j, :], scalar=dcol[:, j:j + 1], in1=ypt[:], op0=ALU.mult, op1=ALU.add),
                     reads=[uB, yptb, sB], writes=[yfB[0]])
                if T == 0 and j == 0:
                    dump("y0", yf[:], [128, TT], F32, [yfB[0]])
                P.op(POOL, lambda e: e.tensor_tensor(out=yf2[:], in0=yf[:], in1=yf[:], op=ALU.mult), reads=[yfB[0]], writes=[yfB[1]])
                P.op(DVE, lambda e: e.tensor_scalar(out=yf2[:], in0=yf2[:], scalar1=0.044715, scalar2=1.0, op0=ALU.mult, op1=ALU.add), reads=[yfB[1]], writes=[yfB[1]])
                P.op(DVE, lambda e: e.tensor_tensor(out=yf2[:], in0=yf2[:], in1=yf[:], op=ALU.mult), reads=[yfB[0], yfB[1]], writes=[yfB[1]])
                P.op(ACT, lambda e: e.activation(out=yf2[:], in_=yf2[:], func=AF.Tanh, scale=0.5 * GELU_C), reads=[yfB[1]], writes=[yfB[1]])
                P.op(DVE, lambda e: e.scalar_tensor_tensor(out=zT[:, j, :], in0=yf2[:], scalar=1.0, in1=yf[:], op0=ALU.add, op1=ALU.mult), reads=[yfB[0], yfB[1]], writes=[zB])

            def first():
                e1(0)
                e1b(0)
                pe_bu(0)
            items.append((4.0, first))

            def group_item_a(k):
                if k + 1 < 32:
                    e1(k + 1)
                e2a(k)
                if k + 1 < 32:
                    e1b(k + 1)

            def group_item_b(k):
                e2b(k)
                if k >= 1:
                    pe_y(k - 1)
                    if (k - 1) % 8 == 7:
                        gelu((k - 1) // 8)
                if k + 1 < 32:
                    pe_bu(k + 1)
            for k in range(32):
                items.append((3.5, lambda k=k: group_item_a(k)))
                items.append((3.5, lambda k=k: group_item_b(k)))

            def tail():
                pe_y(31)
                gelu(3)
            items.append((3.0, tail))

            def glu(m):
                pt, ptb = bank(4 + m % 2)
                proj_fm(wglu, wgB, m, lambda k: zT[:, k, :], 4, [zB], pt, ptb)
                P.op(ACT, lambda e: e.activation(out=g3[:], in_=pt[:], func=AF.Tanh, scale=0.5, bias=bglu[:, m:m + 1]), reads=[ptb, sB], writes=[gB[2]])
                P.op(DVE, lambda e: e.scalar_tensor_tensor(out=sT[hb][:, m, :], in0=g3[:], scalar=1.0, in1=zT[:, m, :], op0=ALU.add, op1=ALU.mult), reads=[zB, gB[2]], writes=[stB_[hb]])
                if T == 0 and m == 3:
                    dump("sT", sT[0][:], [128, 4, TT], BF16, [stB_[0]])
            for m in range(4):
                items.append((2.0, lambda m=m: glu(m)))
            return items

        out_toks = []

        def main_items(T):
            tok0 = T * TT
            hb = T % 2
            items = []

            def loadx():
                for s in range(4):
                    P.dma(xt[:, s, :], x_d[tok0 + s * 128: tok0 + (s + 1) * 128, :], writes=[xB[s]])
            items.append((0.5, loadx))

            def qk(which):
                wslot, wbuf = wload(w_cols(wb_in, 512 * which), (8, 512))
                gcol = qg if which == 0 else kg
                for h in range(4):
                    pt, ptb = nps()
                    proj_fm(wslot, wbuf, h, lambda k: hT[hb][:, k, :], 8, [hB[hb]], pt, ptb)
                    P.op(ACT, lambda e, pt=pt: e.activation(out=g1[:], in_=pt[:], func=AF.Square), reads=[ptb], writes=[gB[0]])
                    p2, p2b = nps()
                    P.op(PE, lambda e, p2=p2: e.matmul(p2[:], lhsT=blk[:], rhs=g1[:], start=True, stop=True), reads=[gB[0], cB], writes=[p2b])
                    rsqrt_from(f1[:], p2[:], 1.0 / 64, [p2b], [fB[0]])
                    if which == 0:
                        dst, dB = qT[:, h, :], [qB]
                    else:
                        dst, dB = Kc[:, h, tok0:tok0 + TT], [KB[T]]
                    P.op(DVE, lambda e, pt=pt, dst=dst: e.scalar_tensor_tensor(out=dst, in0=pt[:], scalar=gcol[:, 0:1], in1=f1[:], op0=ALU.mult, op1=ALU.mult),
                         reads=[ptb, fB[0], sB], writes=dB)
            items.append((12.0, lambda: qk(0)))
            items.append((12.0, lambda: qk(1)))

            def vproj():
                wslot, wbuf = wload(w_cols(wb_in, 1024), (8, 512))
                for s in range(4):
                    pt, ptb = nps()
                    for k in range(8):
                        P.op(PE, lambda e, k=k, s=s, pt=pt: e.matmul(pt[:], lhsT=hT[hb][:, k, s * 128:(s + 1) * 128], rhs=wslot[:, k, :], start=(k == 0), stop=(k == 7)),
                             reads=[wbuf, hB[hb]], writes=[ptb], ms=(k == 7))
                    P.op(ACT, lambda e, pt=pt, s=s: e.activation(out=Vc[:, T * 4 + s, :], in_=pt[:], func=AF.Copy), reads=[ptb], writes=[VB[T]])
                if T == 0:
                    dump("qT", qT[:], [128, 4, TT], BF16, [qB])
                    dump("kT", Kc[:, :, 0:TT], [128, 4, TT], BF16, [KB[0]])
                    dump("V", Vc[:, 0:4, :], [128, 4, 512], BF16, [VB[0]])
            items.append((8.0, vproj))

            nkt = 4 * T + 4
            O = [bank(0), bank(1)]
            Z = [bank(2), bank(3)]
            Sb = [[bank(4), bank(5)], [bank(6), bank(7)]]

            def qrange(kt):
                r = kt - 4 * T
                q0 = max(0, 128 * r)
                return r, q0, TT - q0

            def s_mm(h, kt):
                r, q0, n = qrange(kt)
                for a in range(2):
                    sp, spb = Sb[kt % 2][a]
                    P.op(PE, lambda e, sp=sp, a=a: e.matmul(sp[:, 0:n], lhsT=Kc[64 * a:64 * a + 64, h, kt * 128:(kt + 1) * 128],
                                                            rhs=qT[64 * a:64 * a + 64, h, q0:TT], start=True, stop=True),
                         reads=[KB[kt // 4], qB], writes=[spb], ms=True)

            def att_iter(h, kt):
                r, q0, n = qrange(kt)
                par = kt % 2
                if kt == 0:
                    s_mm(h, 0)
                if kt + 1 < nkt:
                    s_mm(h, kt + 1)
                for a in range(2):
                    sp, spb = Sb[par][a]
                    P.op(ACT, lambda e, sp=sp, a=a: e.activation(out=e_t[par][a][:, 0:n], in_=sp[:, 0:n], func=AF.Exp, scale=0.125), reads=[spb], writes=[eB[par][a]])
                    if r >= 0:
                        P.op(POOL, lambda e, a=a: e.memset(e_t[par][a][64:128, 0:64], 0.0), reads=[], writes=[eB[par][a]])
                lastkt = (kt == nkt - 1)
                for a in range(2):
                    P.op(PE, lambda e, a=a: e.matmul(O[a][0][:, q0:TT], lhsT=Vc[:, kt, h * 128:(h + 1) * 128], rhs=e_t[par][a][:, 0:n], start=(kt == 0), stop=False),
                         reads=[VB[kt // 4], eB[par][a]], writes=[O[a][1]], ms=False)
                    P.op(PE, lambda e, a=a: e.matmul(Z[a][0][:, q0:TT], lhsT=ones[:], rhs=e_t[par][a][:, 0:n], start=(kt == 0), stop=False),
                         reads=[cB, eB[par][a]], writes=[Z[a][1]], ms=(not lastkt))
                    if lastkt:
                        P.op(PE, lambda e, a=a: e.matmul(O[a][0][:], lhsT=zeros[:], rhs=qT[:, h, :], start=False, stop=True),
                             reads=[cB, qB], writes=[O[a][1]], ms=False)
                        P.op(PE, lambda e, a=a: e.matmul(Z[a][0][:], lhsT=zeros[:], rhs=qT[:, h, :], start=False, stop=True),
                             reads=[cB, qB], writes=[Z[a][1]], ms=True)

            def att_fin_a(h):
                for a in range(2):
                    fa, fb = (f1, fB[0]) if a == 0 else (f2, fB[1])
                    P.op(ACT, lambda e, a=a, fa=fa: e.activation(out=fa[:], in_=Z[a][0][:], func=AF.Ln), reads=[Z[a][1]], writes=[fb])
                    P.op(ACT, lambda e, fa=fa: e.activation(out=fa[:], in_=fa[:], func=AF.Exp, scale=-1.0), reads=[fb], writes=[fb])
                    P.op(DVE, lambda e, a=a, fa=fa: e.tensor_tensor(out=fa[:], in0=O[a][0][:], in1=fa[:], op=ALU.mult), reads=[O[a][1], fb], writes=[fb])
                P.op(DVE, lambda e: e.scalar_tensor_tensor(out=f1[:], in0=f2[:], scalar=neglam[:, 0:1], in1=f1[:], op0=ALU.mult, op1=ALU.add),
                     reads=[fB[0], fB[1], sB], writes=[fB[0]])
                P.op(POOL, lambda e: e.tensor_tensor(out=g2[:], in0=f1[:], in1=f1[:], op=ALU.mult), reads=[fB[0]], writes=[gB[1]])

            def att_fin_b(h):
                sp, spb = Sb[0][0]
                P.op(PE, lambda e: e.matmul(sp[:], lhsT=ones[:], rhs=g2[:], start=True, stop=True), reads=[cB, gB[1]], writes=[spb])
                rsqrt_from(f2[:], sp[:], 1.0 / 128, [spb], [fB[1]])
                P.op(DVE, lambda e: e.tensor_tensor(out=aT_[:, h, :], in0=f1[:], in1=f2[:], op=ALU.mult), reads=[fB[0], fB[1]], writes=[aB])
                if T == 0 and h == 3:
                    dump("aT", aT_[:], [128, 4, TT], BF16, [aB])

            for h in range(4):
                for kt in range(nkt):
                    items.append((1.6, lambda h=h, kt=kt: att_iter(h, kt)))
                    if kt == 0 and h >= 1:
                        items.append((2.0, lambda h=h: att_fin_b(h - 1)))
                items.append((2.0, lambda h=h: att_fin_a(h)))
            items.append((2.0, lambda: att_fin_b(3)))
            n_ab = len(items)

            pj = {}

            def projs(i):
                if i == 0:
                    pj["a"] = wload(w_rows(wb_pa, 0), (4, 1024))
                    pj["s"] = wload(w_rows(wb_ps, 0), (4, 1024))
                wpa, wpaB = pj["a"]
                wps_, wpsB = pj["s"]
                pa, pab = nps()
                pss, pssb = nps()
                for k in range(4):
                    P.op(PE, lambda e, k=k: e.matmul(pa[:], lhsT=wpa[:, k, i * 128:(i + 1) * 128], rhs=aT_[:, k, :], start=(k == 0), stop=(k == 3)),
                         reads=[wpaB, aB], writes=[pab], ms=(k == 3))
                for k in range(4):
                    P.op(PE, lambda e, k=k: e.matmul(pss[:], lhsT=wps_[:, k, i * 128:(i + 1) * 128], rhs=sT[hb][:, k, :], start=(k == 0), stop=(k == 3)),
                         reads=[wpsB, stB_[hb]], writes=[pssb], ms=(k == 3))
                P.op(ACT, lambda e: e.activation(out=mrg[:, 8 + i, :], in_=pa[:], func=AF.Copy, scale=0.5), reads=[pab], writes=[mB])
                P.op(DVE, lambda e: e.tensor_scalar(out=mrg[:, i, :], in0=pss[:], scalar1=0.5, scalar2=None, op0=ALU.mult), reads=[pssb], writes=[mB])
            for i in range(8):
                items.append((2.0, lambda i=i: projs(i)))

            gst = {}

            def gates(gsel, half, ii):
                if ii == 0:
                    gst["w"] = wload(w_cols(wb_in, 2048 + 1024 * gsel + 512 * half), (8, 512))
                wslot, wbuf = gst["w"]
                i = half * 4 + ii
                pt, ptb = nps()
                proj_fm(wslot, wbuf, ii, lambda k: hT[hb][:, k, :], 8, [hB[hb]], pt, ptb)
                P.op(ACT, lambda e: e.activation(out=g1[:], in_=pt[:], func=AF.Tanh, scale=0.5, bias=bgate[:, 8 * gsel + i:8 * gsel + i + 1]),
                     reads=[ptb, sB], writes=[gB[0]])
                if gsel == 0:
                    P.op(DVE, lambda e: e.scalar_tensor_tensor(out=mrg[:, 8 + i, :], in0=g1[:], scalar=1.0, in1=mrg[:, 8 + i, :], op0=ALU.add, op1=ALU.mult), reads=[mB, gB[0]], writes=[mB])
                else:
                    P.op(DVE, lambda e: e.scalar_tensor_tensor(out=mrg[:, i, :], in0=g1[:], scalar=1.0, in1=mrg[:, i, :], op0=ALU.add, op1=ALU.mult), reads=[mB, gB[0]], writes=[mB])
                    P.op(POOL, lambda e: e.tensor_tensor(out=mrg[:, i, :], in0=mrg[:, i, :], in1=mrg[:, 8 + i, :], op=ALU.add), reads=[mB], writes=[mB])
                if T == 0 and gsel == 1 and half == 1 and ii == 3:
                    dump("mrg", mrg[:, 0:8, :], [128, 8, TT], BF16, [mB])
            for gsel in range(2):
                for half in range(2):
                    for ii in range(4):
                        items.append((2.0, lambda gsel=gsel, half=half, ii=ii: gates(gsel, half, ii)))

            wst = {}

            def wout(s, half):
                if s == 0 and half == 0:
                    wst["w"] = [wload(w_rows(wb_out, 512 * i), (4, 1024)) for i in range(2)]
                wo = wst["w"]
                pt, ptb = nps()
                for k in range(8):
                    ws_, wb_ = wo[k // 4]
                    P.op(PE, lambda e, k=k, ws_=ws_: e.matmul(pt[:], lhsT=mrg[:, k, s * 128:(s + 1) * 128], rhs=ws_[:, k % 4, half * 512:(half + 1) * 512],
                                                           start=(k == 0), stop=(k == 7)),
                         reads=[wb_, mB], writes=[ptb], ms=(k == 7))
                P.op(DVE, lambda e: e.tensor_tensor(out=xt[:, s, half * 512:(half + 1) * 512], in0=xt[:, s, half * 512:(half + 1) * 512], in1=pt[:], op=ALU.add),
                     reads=[ptb, xB[s]], writes=[xB[s]])
                if T == 0 and s == 3 and half == 1:
                    dump("x1", xt[:], [128, 4, D], F32, xB)
            for s in range(4):
                for half in range(2):
                    items.append((2.0, lambda s=s, half=half: wout(s, half)))

            def norm2(s):
                pt, ptb = nps()
                norm_sub(xt[:, s, :], xB[s], hT[hb], hB[hb], s, pt, ptb)
            for s in range(4):
                items.append((3.0, lambda s=s: norm2(s)))

            mst = {}

            def mlp_in(ffh, q4, ii):
                if ii == 0:
                    mst["i"] = wload(w_cols(wb_mi, ffh * 2048 + q4 * 512), (8, 512))
                wslot, wbuf = mst["i"]
                fc = q4 * 4 + ii
                pt, ptb = nps()
                proj_fm(wslot, wbuf, ii, lambda k: hT[hb][:, k, :], 8, [hB[hb]], pt, ptb)
                rb, rbB = e_t[(fc // 2) % 2][fc % 2], eB[(fc // 2) % 2][fc % 2]
                P.op(ACT, lambda e: e.activation(out=rb[:], in_=pt[:], func=AF.Relu), reads=[ptb], writes=[rbB])
                P.op(POOL, lambda e: e.tensor_tensor(out=mrg[:, fc, :], in0=rb[:], in1=rb[:], op=ALU.mult), reads=[rbB], writes=[mB])

            def mlp_out(ffh, ps_, q4, kk):
                if kk == 0:
                    mst["o"] = wload(w_rows(wb_mo, ffh * 2048 + q4 * 512), (4, 1024))
                wslot, wbuf = mst["o"]
                fc = q4 * 4 + kk
                for s2 in range(2):
                    for half in range(2):
                        pt, ptb = bank(s2 * 2 + half)
                        s = 2 * ps_ + s2
                        P.op(PE, lambda e, s=s, half=half, pt=pt: e.matmul(pt[:], lhsT=mrg[:, fc, s * 128:(s + 1) * 128],
                                                                         rhs=wslot[:, kk, half * 512:(half + 1) * 512], start=(fc == 0), stop=(fc == 15)),
                             reads=[wbuf, mB], writes=[ptb], ms=(fc == 15 or (kk == 3 and s2 == 1 and half == 1)))

            def mlp_evac(ps_):
                for s2 in range(2):
                    for half in range(2):
                        pt, ptb = bank(s2 * 2 + half)
                        s = 2 * ps_ + s2
                        P.op(DVE, lambda e, s=s, half=half, pt=pt: e.tensor_tensor(out=xt[:, s, half * 512:(half + 1) * 512], in0=xt[:, s, half * 512:(half + 1) * 512], in1=pt[:], op=ALU.add),
                             reads=[ptb, xB[s]], writes=[xB[s]])

            for ffh in range(2):
                for q4 in range(4):
                    for ii in range(4):
                        items.append((2.0, lambda ffh=ffh, q4=q4, ii=ii: mlp_in(ffh, q4, ii)))
                for ps_ in range(2):
                    for q4 in range(4):
                        for kk in range(4):
                            items.append((1.0, lambda ffh=ffh, ps_=ps_, q4=q4, kk=kk: mlp_out(ffh, ps_, q4, kk)))
                    items.append((1.0, lambda ps_=ps_: mlp_evac(ps_)))

            def store():
                for s in range(4):
                    ob = Buf()
                    out_toks.append(P.dma(out_d[tok0 + s * 128: tok0 + (s + 1) * 128, :], xt[:, s, :], reads=[xB[s]], writes=[ob]))
            items.append((0.5, store))
            return items[:n_ab], items[n_ab:]

        LEAD = 0.0

        def run_merged(mi, li):
            tm = sum(c for c, _ in mi) or 1.0
            tl = sum(c for c, _ in li) or 1.0
            cm = 0.0
            cl = 0.0
            j = 0
            for c, f in mi:
                f()
                cm += c
                while j < len(li) and cl / tl < cm / tm + LEAD:
                    li[j][1]()
                    cl += li[j][0]
                    j += 1
            while j < len(li):
                li[j][1]()
                j += 1

        INTERLEAVE = True
        run_merged(lookahead_items(0), late_prep)
        P.barrier()
        for T in range(NT):
            ab, cc = main_items(T)
            li = lookahead_items(T + 1) if T + 1 < NT else []
            for c, f in ab[:4]:
                f()
            for c, f in li[:5]:
                f()
            li = li[5:]
            for c, f in ab[4:]:
                f()
            if INTERLEAVE:
                run_merged(cc, li)
            else:
                for c, f in cc:
                    f()
                for c, f in li:
                    f()

        P.wait_tok(SP, out_toks + list(dbg_outs.values()))
        print("instructions recorded:", P.n_instr)
        P.emit()
    return nc


def _consts():
    bf = ml_dtypes.bfloat16
    ident = np.eye(128, dtype=np.float32).astype(bf)
    ones = np.ones((128, 128), np.float32).astype(bf)
    blk = np.zeros((128, 128), np.float32)
    blk[:64, :64] = 1.0
    blk[64:, 64:] = 1.0
    blk = blk.astype(bf)
    iota = np.broadcast_to(np.arange(512, dtype=np.float32)[None, :], (128, 512)).copy()
    rmask = np.zeros((128, 8), np.float32)
    for r in range(128):
        rmask[r, r // 16] = 1.0
    hsel = np.zeros((128, 8), np.float32)
    hsel[:64, 0] = 1.0
    hsel[64:, 1] = 1.0
    hsel[:64, 2] = 1.0
    hsel[64:, 2] = -1.0
    hsel[:, 3] = -math.pi
    hsel[:, 4] = EPS
    hsel[:, 5] = 0.5 * math.pi
    hsel[:, 6] = 0.5
    hsel[:, 7] = 0.25
    return {"c_ident": ident, "c_ones": ones, "c_blk": blk, "c_iota": iota, "c_rmask": rmask, "c_hsel": hsel}


_W_KEYS = ["w_in", "w_mlp_in", "w_mlp_out", "w_out", "w_proj_attn", "w_proj_ssm", "w_glu", "norm_mix_g", "norm_mlp_g",
           "b_gate", "q_norm_g", "k_norm_g", "lambda_q1", "lambda_k1", "lambda_q2", "lambda_k2", "subln_g",
           "ssm_a_re", "ssm_a_im", "ssm_log_dt", "ssm_b_re", "ssm_b_im", "ssm_c_re", "ssm_c_im", "ssm_d", "b_glu"]


def make_in_maps(inputs, n_cores, S):
    shared = _consts()
    for k in _W_KEYS:
        shared[k] = np.ascontiguousarray(np.asarray(inputs[k], dtype=np.float32)[0])
    x = np.asarray(inputs["x"], dtype=np.float32)
    maps = []
    for c in range(n_cores):
        m = dict(shared)
        m["x"] = np.ascontiguousarray(x[c, :S])
        maps.append(m)
    return maps


def kernel(**inputs):
    x = np.asarray(inputs["x"])
    B, S, _ = x.shape
    nc = build(S=S)
    in_maps = make_in_maps(inputs, B, S)
    res = run_bass_kernel_spmd(nc, in_maps, core_ids=list(range(B)))
    out = np.stack([np.asarray(r["out"]) for r in res.results], axis=0)
    return out.astype(np.float32)
```

```python
import math
import numpy as np
import ml_dtypes
import concourse.bass as bass
import concourse.mybir as mybir
from concourse.bass_utils import run_bass_kernel_spmd

F32 = mybir.dt.float32
BF16 = mybir.dt.bfloat16
I32 = mybir.dt.int32
AF = mybir.ActivationFunctionType
ALU = mybir.AluOpType
AX = mybir.AxisListType

D = 1024
NCORES = 8
TT = 512
EPS = 1e-6
TWO_PI = 2.0 * math.pi
TWO_PI_S = 6.2831845
GELU_C = 2.0 * math.sqrt(2.0 / math.pi)
LAM_INIT = 0.8 - 0.6 * math.exp(0.0)

PE, ACT, DVE, POOL, SP = "pe", "act", "dve", "pool", "sp"


class Buf:
    __slots__ = ("w", "r", "name")

    def __init__(self, name=""):
        self.w = None
        self.r = {}
        self.name = name


class Prog:
    NDMA = 40

    def __init__(self, nc):
        self.nc = nc
        self.ops = {k: [] for k in (PE, ACT, DVE, POOL, SP)}
        self.cnt = {k: 0 for k in (PE, ACT, DVE, POOL)}
        self.waited = {k: {} for k in (PE, ACT, DVE, POOL, SP)}
        self.dcnt = [0] * self.NDMA
        self.drr = 0
        self.sems = {}
        self.n_instr = 0

    def alloc_sems(self, es):
        for k in (PE, ACT, DVE, POOL):
            self.sems[k] = es.enter_context(self.nc.semaphore("s_" + k))
        for j in range(self.NDMA):
            self.sems[("d", j)] = es.enter_context(self.nc.semaphore("s_d%d" % j))

    def _deps(self, reads, writes):
        deps = {}

        def add(t):
            if t is not None and deps.get(t[0], 0) < t[1]:
                deps[t[0]] = t[1]
        for b in reads:
            add(b.w)
        for b in writes:
            add(b.w)
            for s, v in b.r.items():
                add((s, v))
        return deps

    def _waits(self, eng, deps):
        out = []
        wd = self.waited[eng]
        for s, v in deps.items():
            if s == PE and eng == PE:
                continue
            if s == PE and v > self.cnt[PE]:
                raise RuntimeError("dependency on un-signalled PE milestone (%s)" % eng)
            if wd.get(s, 0) >= v:
                continue
            wd[s] = v
            out.append((s, v))
        return out

    def _mark(self, tok, reads, writes):
        for b in reads:
            if b.r.get(tok[0], 0) < tok[1]:
                b.r[tok[0]] = tok[1]
        for b in writes:
            b.w = tok
            b.r = {}

    def op(self, eng, fn, reads=(), writes=(), ms=True):
        waits = self._waits(eng, self._deps(reads, writes))
        if ms:
            self.cnt[eng] += 1
            tok = (eng, self.cnt[eng])
            inc = (eng, 1)
        else:
            assert eng == PE
            tok = (eng, self.cnt[eng] + 1)
            inc = None
        self.ops[eng].append((waits, fn, inc))
        self._mark(tok, reads, writes)
        self.n_instr += 1
        return tok

    def dma(self, out, in_, reads=(), writes=(), q=SP, slow=False):
        j = self.drr
        self.drr = (j + 1) % self.NDMA
        deps = self._deps(reads, writes)
        key = ("d", j)
        if self.dcnt[j] > 0:
            v = 16 * self.dcnt[j]
            if deps.get(key, 0) < v:
                deps[key] = v
        waits = self._waits(q, deps)
        self.dcnt[j] += 1
        tok = (key, 16 * self.dcnt[j])
        if slow:
            fn = lambda e: e.dma_start(out=out, in_=in_, allow_slow_non_contiguous=True)
        else:
            fn = lambda e: e.dma_start(out=out, in_=in_)
        self.ops[q].append((waits, fn, (key, 16)))
        self._mark(tok, reads, writes)
        self.n_instr += 1
        return tok

    def barrier(self):
        deps = {k: self.cnt[k] for k in (PE, ACT, DVE, POOL) if self.cnt[k] > 0}
        for j in range(self.NDMA):
            if self.dcnt[j] > 0:
                deps[("d", j)] = 16 * self.dcnt[j]
        for eng in (PE, ACT, DVE, POOL, SP):
            waits = []
            wd = self.waited[eng]
            for s_, v in deps.items():
                if wd.get(s_, 0) >= v:
                    continue
                wd[s_] = v
                waits.append((s_, v))
            if waits:
                self.ops[eng].append((waits, None, None))

    def wait_tok(self, eng, toks):
        deps = {}
        for t in toks:
            if deps.get(t[0], 0) < t[1]:
                deps[t[0]] = t[1]
        waits = self._waits(eng, deps)
        if waits:
            self.ops[eng].append((waits, None, None))

    def emit(self):
        nc = self.nc
        sems = self.sems

        def run(e, lst):
            for waits, fn, inc in lst:
                for s, v in waits:
                    e.wait_ge(sems[s], v)
                if fn is not None:
                    ins = fn(e)
                    if inc is not None:
                        ins.then_inc(sems[inc[0]], inc[1])
        with nc.Block() as block:
            @block.sync
            def _(e):
                run(e, self.ops[SP])

            @block.scalar
            def _(e):
                run(e, self.ops[ACT])

            @block.tensor
            def _(e):
                run(e, self.ops[PE])

            @block.vector
            def _(e):
                run(e, self.ops[DVE])

            @block.gpsimd
            def _(e):
                run(e, self.ops[POOL])


class SbAlloc:
    def __init__(self, nc, limit):
        self.nc = nc
        self.off = 16512
        self.limit = limit
        self.n = 0

    def __call__(self, shape, dt, name=None, at=None):
        esz = 2 if dt == BF16 else 4
        nbytes = esz
        for s in shape[1:]:
            nbytes *= s
        nbytes = (nbytes + 63) // 64 * 64
        if at is None:
            at = self.off
            self.off += nbytes
            assert self.off <= self.limit, "SBUF overflow %d" % self.off
        self.n += 1
        return self.nc.alloc_sbuf_tensor_at(name or ("t%d" % self.n), list(shape), dt, offset=at)


def build(S=4096, debug=None):
    NT = S // TT
    NKT = S // 128
    nc = bass.Bass("TRN2", target_bir_lowering=False)
    P = Prog(nc)

    def din(name, shape, dt=F32):
        return nc.dram_tensor(name, list(shape), dt, kind="ExternalInput").ap()

    x_d = din("x", [S, D])
    w_in_d = din("w_in", [D, 4096])
    w_mi_d = din("w_mlp_in", [D, 4096])
    w_mo_d = din("w_mlp_out", [4096, D])
    w_out_d = din("w_out", [D, D])
    w_pa_d = din("w_proj_attn", [512, D])
    w_ps_d = din("w_proj_ssm", [512, D])
    w_glu_d = din("w_glu", [512, 512])
    g_mix_d = din("norm_mix_g", [D])
    g_mlp_d = din("norm_mlp_g", [D])
    b_gate_d = din("b_gate", [2048])
    qg_d = din("q_norm_g", [64])
    kg_d = din("k_norm_g", [64])
    lq1_d = din("lambda_q1", [64])
    lk1_d = din("lambda_k1", [64])
    lq2_d = din("lambda_q2", [64])
    lk2_d = din("lambda_k2", [64])
    subg_d = din("subln_g", [128])
    are_d = din("ssm_a_re", [32, 64])
    aim_d = din("ssm_a_im", [32, 64])
    ldt_d = din("ssm_log_dt", [32])
    bre_d = din("ssm_b_re", [32, 64, 16])
    bim_d = din("ssm_b_im", [32, 64, 16])
    cre_d = din("ssm_c_re", [32, 16, 64])
    cim_d = din("ssm_c_im", [32, 16, 64])
    dd_d = din("ssm_d", [512])
    bglu_d = din("b_glu", [512])
    ident_d = din("c_ident", [128, 128], BF16)
    ones_d = din("c_ones", [128, 128], BF16)
    blk_d = din("c_blk", [128, 128], BF16)
    iota_d = din("c_iota", [128, 512])
    rmask_d = din("c_rmask", [128, 8])
    hsel_d = din("c_hsel", [128, 8])
    out_d = nc.dram_tensor("out", [S, D], F32, kind="ExternalOutput").ap()

    def dscr(name, shape):
        return nc.dram_tensor(name, list(shape), BF16, kind="Internal").ap()

    wb_in = dscr("wb_in", [D, 4096])
    wb_mi = dscr("wb_mi", [D, 4096])
    wb_mo = dscr("wb_mo", [4096, D])
    wb_out = dscr("wb_out", [D, D])
    wb_pa = dscr("wb_pa", [512, D])
    wb_ps = dscr("wb_ps", [512, D])
    wb_glu = dscr("wb_glu", [512, 512])

    dbg_outs = {}

    def dump(name, src_ap, shape, dt, reads):
        if debug is None or name not in debug:
            return
        t = nc.dram_tensor("dbg_" + name, list(shape), dt, kind="ExternalOutput").ap()
        dbg_outs[name] = P.dma(t, src_ap, reads=reads, writes=[Buf()])

    from contextlib import ExitStack
    with ExitStack() as es:
        P.alloc_sems(es)
        sb = SbAlloc(nc, 229376)
        psum = [es.enter_context(nc.psum_tensor("ps%d" % i, [128, 512], F32)) for i in range(8)]
        psb = [Buf("ps%d" % i) for i in range(8)]

        ident = sb([128, 128], BF16, "ident")
        ones = sb([128, 128], BF16, "ones")
        blk = sb([128, 128], BF16, "blk")
        iota = sb([128, 512], F32, "iota")
        rmask = sb([128, 8], F32, "rmask")
        hsel = sb([128, 8], F32, "hsel")
        cB = Buf("consts")
        P.dma(ident[:], ident_d, writes=[cB])
        P.dma(ones[:], ones_d, writes=[cB])
        P.dma(blk[:], blk_d, writes=[cB])
        P.dma(iota[:], iota_d, writes=[cB])
        P.dma(rmask[:], rmask_d, writes=[cB])
        P.dma(hsel[:], hsel_d, writes=[cB])
        negpi = hsel[:, 3:4]
        zeros = sb([128, 128], BF16, "zeros")
        P.op(POOL, lambda e: e.memset(zeros[:], 0.0), writes=[cB])
        epsc = hsel[:, 4:5]
        hpi = hsel[:, 5:6]
        halfc = hsel[:, 6:7]
        quartc = hsel[:, 7:8]

        gmix = sb([128, 8], F32, "gmix")
        gmlp = sb([128, 8], F32, "gmlp")
        gsub = sb([128, 1], F32, "gsub")
        bgate = sb([128, 16], F32, "bgate")
        bglu = sb([128, 4], F32, "bglu")
        dcol = sb([128, 4], F32, "dcol")
        qg = sb([128, 1], F32, "qg")
        kg = sb([128, 1], F32, "kg")
        lamt = sb([128, 4, 64], F32, "lamt", at=150 * 1024)
        neglam = sb([128, 1], F32, "neglam")
        sB = Buf("smallparams")
        P.dma(gmix[:], g_mix_d.rearrange("(k p) -> p k", p=128), writes=[sB], slow=True)
        P.dma(gmlp[:], g_mlp_d.rearrange("(k p) -> p k", p=128), writes=[sB], slow=True)
        P.dma(gsub[:], subg_d.rearrange("(p o) -> p o", o=1), writes=[sB], slow=True)
        P.dma(bgate[:], b_gate_d.rearrange("(k p) -> p k", p=128), writes=[sB], slow=True)
        P.dma(bglu[:], bglu_d.rearrange("(k p) -> p k", p=128), writes=[sB], slow=True)
        P.dma(dcol[:], dd_d.rearrange("(k p) -> p k", p=128), writes=[sB], slow=True)
        for h in range(2):
            P.dma(qg[64 * h:64 * h + 64, :], qg_d.rearrange("(p o) -> p o", o=1), writes=[sB], slow=True)
            P.dma(kg[64 * h:64 * h + 64, :], kg_d.rearrange("(p o) -> p o", o=1), writes=[sB], slow=True)
        for i, t in enumerate((lq1_d, lk1_d, lq2_d, lk2_d)):
            P.dma(lamt[:, i, :], t.partition_broadcast(128), writes=[sB])
        lsum = sb([128, 2], F32, "lsum", at=150 * 1024 + 1024)
        lprod = sb([128, 2, 64], F32, "lprod", at=150 * 1024 + 1024 + 64)
        P.op(DVE, lambda e: e.tensor_tensor(out=lprod[:, 0, :], in0=lamt[:, 0, :], in1=lamt[:, 1, :], op=ALU.mult), reads=[sB], writes=[sB])
        P.op(DVE, lambda e: e.tensor_tensor(out=lprod[:, 1, :], in0=lamt[:, 2, :], in1=lamt[:, 3, :], op=ALU.mult), reads=[sB], writes=[sB])
        P.op(DVE, lambda e: e.reduce_sum(out=lsum[:, 0:1], in_=lprod[:, 0, :], axis=AX.X), reads=[sB], writes=[sB])
        P.op(DVE, lambda e: e.reduce_sum(out=lsum[:, 1:2], in_=lprod[:, 1, :], axis=AX.X), reads=[sB], writes=[sB])
        P.op(ACT, lambda e: e.activation(out=lsum[:], in_=lsum[:], func=AF.Exp), reads=[sB], writes=[sB])
        P.op(DVE, lambda e: e.tensor_tensor(out=neglam[:], in0=lsum[:, 1:2], in1=lsum[:, 0:1], op=ALU.subtract), reads=[sB], writes=[sB])
        P.op(DVE, lambda e: e.tensor_scalar(out=neglam[:], in0=neglam[:], scalar1=-LAM_INIT, scalar2=None, op0=ALU.add), reads=[sB], writes=[sB])
        P.op(DVE, lambda e: e.tensor_scalar(out=bgate[:], in0=bgate[:], scalar1=0.5, scalar2=None, op0=ALU.mult), reads=[sB], writes=[sB])
        P.op(DVE, lambda e: e.tensor_scalar(out=bglu[:], in0=bglu[:], scalar1=0.5, scalar2=None, op0=ALU.mult), reads=[sB], writes=[sB])
        P.op(DVE, lambda e: e.tensor_scalar(out=gsub[:], in0=gsub[:], scalar1=1.0 - LAM_INIT, scalar2=None, op0=ALU.mult), reads=[sB], writes=[sB])

        rho = sb([128, 32], F32, "rho")
        thr = sb([128, 32], F32, "thr")
        toff = sb([128, 32], F32, "toff")
        ki32 = sb([128, 32], I32, "ki32")
        BBt = sb([128, 4, 2, 128], BF16, "BBt")
        CC = sb([128, 32, 2, 64], BF16, "CC")
        wglu = sb([128, 4, 512], BF16, "wglu")
        carry = sb([128, 32], F32, "carry")

        prep_mark = sb.off
        NST = 5
        stf = [sb([128, 2048], F32, "stf%d" % i) for i in range(NST)]
        stb = [sb([128, 2048], BF16, "stb%d" % i) for i in range(NST)]
        stfB = [Buf() for _ in range(NST)]
        stbB = [Buf() for _ in range(NST)]
        wscrB = Buf("wscratch")
        pi = [0]

        def prep(src, dst, R, C, scale):
            cw = min(C, 2048)
            out = []
            for kc in range(R // 128):
                for c0 in range(0, C, cw):
                    out.append((5.0, lambda kc=kc, c0=c0: prep_chunk(src, dst, cw, scale, kc, c0)))
            return out

        def prep_chunk(src, dst, cw, scale, kc, c0):
            if True:
                if True:
                    i = pi[0] % NST
                    pi[0] += 1
                    P.dma(stf[i][:, 0:cw], src[kc * 128:(kc + 1) * 128, c0:c0 + cw], writes=[stfB[i]])
                    sc = scale(kc) if scale is not None else None
                    if pi[0] % 2 == 0:
                        qeng = ACT
                        if sc is None:
                            P.op(ACT, lambda e, i=i: e.activation(out=stb[i][:, 0:cw], in_=stf[i][:, 0:cw], func=AF.Copy),
                                 reads=[stfB[i]], writes=[stbB[i]])
                        else:
                            P.op(ACT, lambda e, i=i, sc=sc: e.activation(out=stb[i][:, 0:cw], in_=stf[i][:, 0:cw], func=AF.Copy, scale=sc),
                                 reads=[stfB[i], sB, cB], writes=[stbB[i]])
                    else:
                        qeng = ACT
                        if sc is None:
                            P.op(DVE, lambda e, i=i: e.tensor_copy(out=stb[i][:, 0:cw], in_=stf[i][:, 0:cw]),
                                 reads=[stfB[i]], writes=[stbB[i]])
                        else:
                            P.op(DVE, lambda e, i=i, sc=sc: e.tensor_scalar(out=stb[i][:, 0:cw], in0=stf[i][:, 0:cw], scalar1=sc, scalar2=None, op0=ALU.mult),
                                 reads=[stfB[i], sB, cB], writes=[stbB[i]])
                    P.dma(dst[kc * 128:(kc + 1) * 128, c0:c0 + cw], stb[i][:, 0:cw], reads=[stbB[i]], writes=[wscrB], q=qeng)

        for c_, f_ in prep(w_in_d, wb_in, D, 4096, lambda kc: gmix[:, kc:kc + 1]) + prep(w_glu_d, wb_glu, 512, 512, lambda kc: halfc):
            f_()
        late_prep = (prep(w_pa_d, wb_pa, 512, D, lambda kc: gsub[:, 0:1]) + prep(w_ps_d, wb_ps, 512, D, lambda kc: quartc)
                     + prep(w_out_d, wb_out, D, D, None) + prep(w_mi_d, wb_mi, D, 4096, lambda kc: gmlp[:, kc:kc + 1])
                     + prep(w_mo_d, wb_mo, 4096, D, None))
        prep_end = sb.off

        sb.off = prep_end
        tB = Buf("ssm_tables")
        ar = sb([128, 32], F32)
        ai = sb([128, 32], F32)
        dt = sb([128, 32], F32)
        for h in range(2):
            P.dma(ar[64 * h:64 * h + 64, :], are_d.rearrange("g p -> p g"), writes=[tB], slow=True)
            P.dma(ai[64 * h:64 * h + 64, :], aim_d.rearrange("g p -> p g"), writes=[tB], slow=True)
        P.dma(dt[:], ldt_d.partition_broadcast(128), writes=[tB])
        t0 = sb([128, 32], F32)
        t1 = sb([128, 32], F32)
        t2 = sb([128, 32], F32)
        t3 = sb([128, 32], F32)
        fre = sb([128, 32], F32)
        fim = sb([128, 32], F32)

        def dv(fn):
            P.op(DVE, fn, reads=[tB, cB], writes=[tB])

        def ac(fn):
            P.op(ACT, fn, reads=[tB, cB], writes=[tB])

        TT_ = lambda e, o, a, b, op: e.tensor_tensor(out=o, in0=a, in1=b, op=op)
        ac(lambda e: e.activation(out=dt[:], in_=dt[:], func=AF.Exp))
        dv(lambda e: TT_(e, t0[:], ar[:], dt[:], ALU.mult))
        ac(lambda e: e.activation(out=rho[:], in_=t0[:], func=AF.Exp))
        dv(lambda e: TT_(e, t1[:], ai[:], dt[:], ALU.mult))
        dv(lambda e: e.tensor_scalar(out=t1[:], in0=t1[:], scalar1=1.0 / TWO_PI, scalar2=None, op0=ALU.mult))
        dv(lambda e: e.tensor_copy(out=ki32[:], in_=t1[:]))
        dv(lambda e: TT_(e, thr[:], t1[:], ki32[:], ALU.subtract))
        ac(lambda e: e.activation(out=t2[:], in_=thr[:], func=AF.Abs))
        ac(lambda e: e.activation(out=t1[:], in_=thr[:], func=AF.Sin, scale=TWO_PI_S))
        ac(lambda e: e.activation(out=t2[:], in_=t2[:], func=AF.Sin, scale=-TWO_PI_S, bias=hpi))
        dv(lambda e: TT_(e, t3[:], t2[:], rho[:], ALU.mult))
        dv(lambda e: TT_(e, t1[:], t1[:], rho[:], ALU.mult))
        dv(lambda e: e.tensor_scalar(out=t3[:], in0=t3[:], scalar1=-1.0, scalar2=None, op0=ALU.add))
        dv(lambda e: TT_(e, t0[:], ar[:], ar[:], ALU.mult))
        dv(lambda e: TT_(e, t2[:], ai[:], ai[:], ALU.mult))
        dv(lambda e: TT_(e, t0[:], t0[:], t2[:], ALU.add))
        dv(lambda e: e.reciprocal(out=t0[:], in_=t0[:]))
        dv(lambda e: TT_(e, fre[:], t3[:], ar[:], ALU.mult))
        dv(lambda e: TT_(e, t2[:], t1[:], ai[:], ALU.mult))
        dv(lambda e: TT_(e, fre[:], fre[:], t2[:], ALU.add))
        dv(lambda e: TT_(e, fre[:], fre[:], t0[:], ALU.mult))
        dv(lambda e: TT_(e, fim[:], t1[:], ar[:], ALU.mult))
        dv(lambda e: TT_(e, t2[:], t3[:], ai[:], ALU.mult))
        dv(lambda e: TT_(e, fim[:], fim[:], t2[:], ALU.subtract))
        dv(lambda e: TT_(e, fim[:], fim[:], t0[:], ALU.mult))
        bnr = sb([128, 32, 16], F32)
        bni = sb([128, 32, 16], F32)
        for h in range(2):
            P.dma(bnr[64 * h:64 * h + 64, :, :], bre_d.rearrange("g p m -> p g m"), writes=[tB], slow=True)
            P.dma(bni[64 * h:64 * h + 64, :, :], bim_d.rearrange("g p m -> p g m"), writes=[tB], slow=True)
        bbr = sb([128, 32, 16], F32)
        bbi = sb([128, 32, 16], F32)
        tmpb = sb([128, 32, 16], F32)
        freb = fre[:].unsqueeze(2).to_broadcast([128, 32, 16])
        fimb = fim[:].unsqueeze(2).to_broadcast([128, 32, 16])
        dv(lambda e: TT_(e, bbr[:], bnr[:], freb, ALU.mult))
        dv(lambda e: TT_(e, tmpb[:], bni[:], fimb, ALU.mult))
        dv(lambda e: TT_(e, bbr[:], bbr[:], tmpb[:], ALU.subtract))
        dv(lambda e: TT_(e, bbi[:], bni[:], freb, ALU.mult))
        dv(lambda e: TT_(e, tmpb[:], bnr[:], fimb, ALU.mult))
        dv(lambda e: TT_(e, bbi[:], bbi[:], tmpb[:], ALU.add))
        nnat = sb([128, 32, 16], BF16)
        nsw = sb([128, 32, 16], BF16)
        dv(lambda e: e.tensor_scalar(out=tmpb[:], in0=bbr[:], scalar1=hsel[:, 0:1], scalar2=None, op0=ALU.mult))
        dv(lambda e: e.scalar_tensor_tensor(out=nnat[:], in0=bbi[:], scalar=hsel[:, 1:2], in1=tmpb[:], op0=ALU.mult, op1=ALU.add))
        dv(lambda e: e.tensor_scalar(out=tmpb[:], in0=bbi[:], scalar1=hsel[:, 0:1], scalar2=None, op0=ALU.mult))
        dv(lambda e: e.scalar_tensor_tensor(out=bbr[:], in0=bbr[:], scalar=hsel[:, 1:2], in1=tmpb[:], op0=ALU.mult, op1=ALU.subtract))
        dv(lambda e: e.tensor_scalar(out=nsw[:], in0=bbr[:], scalar1=-1.0, scalar2=None, op0=ALU.mult))
        bbB = Buf("BB")
        for j in range(4):
            for v, src in enumerate((nnat, nsw)):
                pt = psum[(2 * j + v) % 8]
                ptb = psb[(2 * j + v) % 8]
                pv = pt[:].bitcast(BF16)[:, 0:128]
                P.op(PE, lambda e, pv=pv, src=src, j=j: e.transpose(out=pv, in_=src[:, 8 * j:8 * j + 8, :].rearrange("p g m -> p (g m)"), identity=ident[:]),
                     reads=[tB, cB], writes=[ptb])
                P.op(DVE, lambda e, pv=pv, j=j, v=v: e.tensor_copy(out=BBt[:, j, v, :], in_=pv), reads=[ptb], writes=[bbB])
        cst = sb([128, 4, 2, 64], F32)
        cstb = sb([128, 4, 2, 2, 64], BF16)
        P.dma(cst[:, :, 0, :], cre_d.rearrange("(j g) n p -> (g n) j p", j=4), writes=[tB])
        P.dma(cst[:, :, 1, :], cim_d.rearrange("(j g) n p -> (g n) j p", j=4), writes=[tB])
        dv(lambda e: e.tensor_copy(out=cstb[:, :, 0, 0, :], in_=cst[:, :, 0, :]))
        dv(lambda e: e.tensor_scalar(out=cstb[:, :, 0, 1, :], in0=cst[:, :, 1, :], scalar1=-1.0, scalar2=None, op0=ALU.mult))
        dv(lambda e: e.tensor_scalar(out=cstb[:, :, 1, 0, :], in0=cst[:, :, 1, :], scalar1=-1.0, scalar2=None, op0=ALU.mult))
        dv(lambda e: e.tensor_scalar(out=cstb[:, :, 1, 1, :], in0=cst[:, :, 0, :], scalar1=-1.0, scalar2=None, op0=ALU.mult))
        ccB = Buf("CC")
        P.op(POOL, lambda e: e.memset(CC[:], 0.0), writes=[ccB])
        for j in range(4):
            for v in range(2):
                pt = psum[(2 * j + v) % 8]
                ptb = psb[(2 * j + v) % 8]
                pv = pt[:].bitcast(BF16)[:, 0:128]
                P.op(PE, lambda e, pv=pv, j=j, v=v: e.transpose(out=pv, in_=cstb[:, j, v, :, :].rearrange("p h q -> p (h q)"), identity=ident[:]),
                     reads=[tB, cB], writes=[ptb])
                for gp in range(8):
                    P.op(DVE, lambda e, pv=pv, j=j, v=v, gp=gp: e.tensor_copy(out=CC[:, 8 * j + gp, v, 16 * (gp % 4):16 * (gp % 4) + 16], in_=pv[:, 16 * gp:16 * gp + 16]),
                         reads=[ptb], writes=[ccB])
        wgB = Buf("wglu")
        P.dma(wglu[:], wb_glu.rearrange("(k p) c -> p k c", p=128), reads=[wscrB], writes=[wgB])
        carB = Buf("carry")
        P.op(POOL, lambda e: e.memset(carry[:], 0.0), writes=[carB])
        temps_end = sb.off
        sb.off = prep_mark
        Kc = sb([128, 4, S], BF16, "Kc")
        Vc = sb([128, NKT, 512], BF16, "Vc")
        KB = [Buf("K%d" % t) for t in range(NT)]
        VB = [Buf("V%d" % t) for t in range(NT)]
        xt = sb([128, 4, D], F32, "xt")
        xB = [Buf("x%d" % s) for s in range(4)]
        sb.off = max(sb.off, temps_end)
        xst = sb([128, D], F32, "xst")
        xstB = Buf("xst")
        hT = [sb([128, 8, TT], BF16, "hT%d" % i) for i in range(2)]
        hB = [Buf("hT0"), Buf("hT1")]
        xn = sb([128, D], BF16, "xn")
        xnB = Buf("xn")
        nstat = sb([128, 8], F32, "nstat")
        nsB = Buf("nstat")
        qT = sb([128, 4, TT], BF16, "qT")
        qB = Buf("qT")
        uT = sb([128, 4, TT], BF16, "uT")
        uB = Buf("uT")
        zT = sb([128, 4, TT], BF16, "zT")
        zB = Buf("zT")
        sT = [sb([128, 4, TT], BF16, "sT%d" % i) for i in range(2)]
        stB_ = [Buf("sT0"), Buf("sT1")]
        aT_ = sb([128, 4, TT], BF16, "aT")
        aB = Buf("aT")
        mrg = sb([128, 16, TT], BF16, "mrg_ff")
        mB = Buf("mrg_ff")
        e_t = [[sb([128, TT], BF16, "e%d%d" % (i, j)) for j in range(2)] for i in range(2)]
        eB = [[Buf() for j in range(2)] for i in range(2)]
        f1 = sb([128, TT], F32, "f1")
        f2 = sb([128, TT], F32, "f2")
        fB = [Buf("f1"), Buf("f2")]
        g1 = sb([128, TT], BF16, "g1")
        g2 = sb([128, TT], BF16, "g2")
        g3 = sb([128, TT], BF16, "g3")
        gB = [Buf("g1"), Buf("g2"), Buf("g3")]
        nS = [sb([128, TT], F32, "nS%d" % i) for i in range(2)]
        nC = [sb([128, TT], F32, "nC%d" % i) for i in range(2)]
        zb = [sb([128, TT], F32, "zb%d" % i) for i in range(2)]
        zt2 = sb([128, TT], F32, "zt2")
        yf2 = zt2
        v12 = [sb([128, 2, TT], BF16, "v12_%d" % i) for i in range(2)]
        um = [sb([128, TT], BF16, "um%d" % i) for i in range(2)]
        nSB = [Buf() for _ in range(2)]
        nCB = [Buf() for _ in range(2)]
        zbB = [Buf() for _ in range(2)]
        zt2B = Buf()
        yf = f1
        yfB = [fB[0], zt2B]
        vB = [Buf() for _ in range(2)]
        umB = [Buf() for _ in range(2)]
        NSL = 2
        wsl = [sb([128, 4096], BF16, "wsl%d" % i) for i in range(NSL)]
        wslB = [Buf("wsl%d" % i) for i in range(NSL)]
        print("SBUF bytes used per partition:", sb.off)

        wstate = {"i": 0}

        def wload(src_ap, view):
            i = wstate["i"] % NSL
            wstate["i"] += 1
            a, b = view
            dst = wsl[i][:].rearrange("p (a b) -> p a b", a=a)
            P.dma(dst, src_ap, reads=[wscrB], writes=[wslB[i]])
            return dst, wslB[i]

        def w_cols(wb, c0):
            return wb[:, c0:c0 + 512].rearrange("(k p) c -> p k c", p=128)

        def w_rows(wb, r0):
            return wb[r0:r0 + 512, :].rearrange("(k p) c -> p k c", p=128)

        psi = [0]

        def nps():
            i = psi[0] % 4
            psi[0] += 1
            return psum[i], psb[i]

        def bank(i):
            return psum[i], psb[i]

        def norm_sub(src_ap, srcB, hdst, hdstB, s, pt, ptb):
            P.op(POOL, lambda e: e.memset(nstat[:, s:s + 1], 0.0), writes=[nsB])
            P.op(ACT, lambda e: e.activation(out=xn[:], in_=src_ap, func=AF.Square, accum_out=nstat[:, s:s + 1]),
                 reads=[srcB], writes=[xnB, nsB])
            P.op(ACT, lambda e: e.activation(out=nstat[:, 4 + s:5 + s], in_=nstat[:, s:s + 1], func=AF.Ln, scale=1.0 / D, bias=epsc), reads=[nsB, cB], writes=[nsB])
            P.op(ACT, lambda e: e.activation(out=nstat[:, 4 + s:5 + s], in_=nstat[:, 4 + s:5 + s], func=AF.Exp, scale=-0.5), reads=[nsB], writes=[nsB])
            P.op(DVE, lambda e: e.tensor_scalar(out=xn[:], in0=src_ap, scalar1=nstat[:, 4 + s:5 + s], scalar2=None, op0=ALU.mult),
                 reads=[srcB, nsB], writes=[xnB])
            pv = pt[:].bitcast(BF16)
            for dk in range(8):
                P.op(PE, lambda e, dk=dk: e.transpose(out=pv[:, dk * 128:(dk + 1) * 128], in_=xn[:, dk * 128:(dk + 1) * 128], identity=ident[:]),
                     reads=[xnB, cB], writes=[ptb], ms=(dk == 7))
            P.op(ACT, lambda e: e.activation(out=hdst[:, :, s * 128:(s + 1) * 128], in_=pv.rearrange("p (k t) -> p k t", k=8), func=AF.Copy),
                 reads=[ptb], writes=[hdstB])

        def proj_fm(wslot, wbuf, ct, rhs_fn, nk, rbufs, pt, ptb):
            for k in range(nk):
                P.op(PE, lambda e, k=k: e.matmul(pt[:], lhsT=wslot[:, k, ct * 128:(ct + 1) * 128], rhs=rhs_fn(k), start=(k == 0), stop=(k == nk - 1)),
                     reads=[wbuf] + rbufs, writes=[ptb], ms=(k == nk - 1))

        def rsqrt_from(out_ap, in_ap, inv_n, rbufs, wbufs):
            P.op(ACT, lambda e: e.activation(out=out_ap, in_=in_ap, func=AF.Ln, scale=inv_n, bias=epsc), reads=rbufs + [cB], writes=wbufs)
            P.op(ACT, lambda e: e.activation(out=out_ap, in_=out_ap, func=AF.Exp, scale=-0.5), reads=wbufs, writes=wbufs)

        def lookahead_items(T):
            tok0 = T * TT
            hb = T % 2
            items = []

            def front(s):
                P.dma(xst[:], x_d[tok0 + s * 128: tok0 + (s + 1) * 128, :], writes=[xstB])
                pt, ptb = bank(4 + s % 3)
                norm_sub(xst[:], xstB, hT[hb], hB[hb], s, pt, ptb)
            for s in range(4):
                items.append((3.0, lambda s=s: front(s)))

            def uproj():
                wslot, wbuf = wload(w_cols(wb_in, 1536), (8, 512))
                for j in range(4):
                    pt, ptb = bank(4 + j % 3)
                    proj_fm(wslot, wbuf, j, lambda k: hT[hb][:, k, :], 8, [hB[hb]], pt, ptb)
                    P.op(ACT, lambda e, pt=pt, j=j: e.activation(out=uT[:, j, :], in_=pt[:], func=AF.Copy), reads=[ptb], writes=[uB])
                if T == 0:
                    dump("hT", hT[0][:], [128, 8, TT], BF16, [hB[0]])
                    dump("uT", uT[:], [128, 4, TT], BF16, [uB])
                P.op(DVE, lambda e: e.tensor_scalar(out=toff[:], in0=thr[:], scalar1=float(tok0), scalar2=None, op0=ALU.mult), reads=[tB], writes=[tB])
                P.op(DVE, lambda e: e.tensor_copy(out=ki32[:], in_=toff[:]), reads=[tB], writes=[tB])
                P.op(DVE, lambda e: e.tensor_tensor(out=toff[:], in0=toff[:], in1=ki32[:], op=ALU.subtract), reads=[tB], writes=[tB])
            items.append((8.0, uproj))

            pn, pnb = bank(4)
            psw, pswb = bank(5)

            def e1(g):
                j, gp, tb = g // 8, g % 8, g % 2
                P.op(ACT, lambda e: e.activation(out=um[tb][:], in_=uT[:, j, :], func=AF.Copy, scale=rmask[:, gp:gp + 1]), reads=[uB, cB], writes=[umB[tb]])
                P.op(ACT, lambda e: e.activation(out=nS[tb][:], in_=iota[:], func=AF.Identity, scale=thr[:, g:g + 1], bias=toff[:, g:g + 1]),
                     reads=[tB, cB], writes=[nSB[tb]])

            def e1b(g):
                tb = g % 2
                nci = nC[tb][:].bitcast(I32)
                P.op(DVE, lambda e: e.tensor_copy(out=nci, in_=nS[tb][:]), reads=[nSB[tb]], writes=[nCB[tb]])
                P.op(DVE, lambda e: e.tensor_copy(out=nC[tb][:], in_=nci), reads=[nCB[tb]], writes=[nCB[tb]])
                P.op(DVE, lambda e: e.tensor_tensor(out=nS[tb][:], in0=nS[tb][:], in1=nC[tb][:], op=ALU.subtract), reads=[nCB[tb], nSB[tb]], writes=[nSB[tb]])
                P.op(ACT, lambda e: e.activation(out=nC[tb][:], in_=nS[tb][:], func=AF.Abs), reads=[nSB[tb]], writes=[nCB[tb]])
                P.op(ACT, lambda e: e.activation(out=nC[tb][:], in_=nC[tb][:], func=AF.Sin, scale=-TWO_PI_S, bias=hpi), reads=[nCB[tb], cB], writes=[nCB[tb]])
                P.op(ACT, lambda e: e.activation(out=nS[tb][:], in_=nS[tb][:], func=AF.Sin, scale=TWO_PI_S), reads=[nSB[tb]], writes=[nSB[tb]])

            def pe_bu(g):
                j, tb = g // 8, g % 2
                P.op(PE, lambda e: e.matmul(pn[:], lhsT=BBt[:, j, 0, :], rhs=um[tb][:], start=True, stop=True), reads=[bbB, umB[tb]], writes=[pnb])
                P.op(PE, lambda e: e.matmul(psw[:], lhsT=BBt[:, j, 1, :], rhs=um[tb][:], start=True, stop=True), reads=[bbB, umB[tb]], writes=[pswb])

            def e2a(g):
                tb = g % 2
                P.op(DVE, lambda e: e.tensor_tensor(out=zb[tb][:], in0=pn[:], in1=nC[tb][:], op=ALU.mult), reads=[pnb, nCB[tb]], writes=[zbB[tb]])
                P.op(DVE, lambda e: e.tensor_tensor(out=zt2[:], in0=psw[:], in1=nS[tb][:], op=ALU.mult), reads=[pswb, nSB[tb]], writes=[zt2B])
                P.op(DVE, lambda e: e.tensor_tensor(out=zb[tb][:], in0=zb[tb][:], in1=zt2[:], op=ALU.add), reads=[zt2B, zbB[tb]], writes=[zbB[tb]])

            def e2b(g):
                tb = g % 2
                P.op(DVE, lambda e: e.tensor_tensor_scan(out=zb[tb][:], data0=rho[:, g:g + 1].to_broadcast([128, TT]), data1=zb[tb][:], initial=carry[:, g:g + 1], op0=ALU.mult, op1=ALU.add),
                     reads=[zbB[tb], tB, carB], writes=[zbB[tb]])
                P.op(DVE, lambda e: e.tensor_copy(out=carry[:, g:g + 1], in_=zb[tb][:, TT - 1:TT]), reads=[zbB[tb]], writes=[carB])
                P.op(DVE, lambda e: e.tensor_tensor(out=v12[tb][:, 0, :], in0=zb[tb][:], in1=nC[tb][:], op=ALU.mult), reads=[zbB[tb], nCB[tb]], writes=[vB[tb]])
                P.op(POOL, lambda e: e.tensor_tensor(out=v12[tb][:, 1, :], in0=zb[tb][:], in1=nS[tb][:], op=ALU.mult), reads=[zbB[tb], nSB[tb]], writes=[vB[tb]])

            def pe_y(g):
                j, gp, tb = g // 8, g % 8, g % 2
                hh = gp // 4
                ypt, yptb = bank(6)
                for v in range(2):
                    P.op(PE, lambda e, v=v: e.matmul(ypt[64 * hh:64 * hh + 64, :], lhsT=CC[:, g, v, :], rhs=v12[tb][:, v, :],
                                                     start=(gp % 4 == 0 and v == 0), stop=(gp % 4 == 3 and v == 1)),
                         reads=[ccB, vB[tb]], writes=[yptb], ms=True)

            def gelu(j):
                ypt, yptb = bank(6)
                P.op(DVE, lambda e: e.scalar_tensor_tensor(out=yf[:], in0=uT[:, j, :], scalar=dcol[:, j:j + 1], in1=ypt[:], op0=ALU.mult, op1=ALU.add),
                     reads=[uB, yptb, sB], writes=[yfB[0]])
                if T == 0 and j == 0:
                    dump("y0", yf[:], [128, TT], F32, [yfB[0]])
                P.op(POOL, lambda e: e.tensor_tensor(out=yf2[:], in0=yf[:], in1=yf[:], op=ALU.mult), reads=[yfB[0]], writes=[yfB[1]])
                P.op(DVE, lambda e: e.tensor_scalar(out=yf2[:], in0=yf2[:], scalar1=0.044715, scalar2=1.0, op0=ALU.mult, op1=ALU.add), reads=[yfB[1]], writes=[yfB[1]])
                P.op(DVE, lambda e: e.tensor_tensor(out=yf2[:], in0=yf2[:], in1=yf[:], op=ALU.mult), reads=[yfB[0], yfB[1]], writes=[yfB[1]])
                P.op(ACT, lambda e: e.activation(out=yf2[:], in_=yf2[:], func=AF.Tanh, scale=0.5 * GELU_C), reads=[yfB[1]], writes=[yfB[1]])
                P.op(DVE, lambda e: e.scalar_tensor_tensor(out=zT[:, j, :], in0=yf2[:], scalar=1.0, in1=yf[:], op0=ALU.add, op1=ALU.mult), reads=[yfB[0], yfB[1]], writes=[zB])

            def first():
                e1(0)
                e1b(0)
                pe_bu(0)
            items.append((4.0, first))

            def group_item(k):
                if k + 1 < 32:
                    e1(k + 1)
                e2a(k)
                if k + 1 < 32:
                    e1b(k + 1)
                e2b(k)
                if k >= 1:
                    pe_y(k - 1)
                    if (k - 1) % 8 == 7:
                        gelu((k - 1) // 8)
                if k + 1 < 32:
                    pe_bu(k + 1)
            for k in range(32):
                items.append((7.0, lambda k=k: group_item(k)))

            def tail():
                pe_y(31)
                gelu(3)
            items.append((3.0, tail))

            def glu(m):
                pt, ptb = bank(4 + m % 2)
                proj_fm(wglu, wgB, m, lambda k: zT[:, k, :], 4, [zB], pt, ptb)
                P.op(ACT, lambda e: e.activation(out=g3[:], in_=pt[:], func=AF.Tanh, scale=0.5, bias=bglu[:, m:m + 1]), reads=[ptb, sB], writes=[gB[2]])
                P.op(DVE, lambda e: e.scalar_tensor_tensor(out=sT[hb][:, m, :], in0=g3[:], scalar=1.0, in1=zT[:, m, :], op0=ALU.add, op1=ALU.mult), reads=[zB, gB[2]], writes=[stB_[hb]])
                if T == 0 and m == 3:
                    dump("sT", sT[0][:], [128, 4, TT], BF16, [stB_[0]])
            for m in range(4):
                items.append((2.0, lambda m=m: glu(m)))
            return items

        out_toks = []

        def main_items(T):
            tok0 = T * TT
            hb = T % 2
            items = []

            def loadx():
                for s in range(4):
                    P.dma(xt[:, s, :], x_d[tok0 + s * 128: tok0 + (s + 1) * 128, :], writes=[xB[s]])
            items.append((0.5, loadx))

            def qk(which):
                wslot, wbuf = wload(w_cols(wb_in, 512 * which), (8, 512))
                gcol = qg if which == 0 else kg
                for h in range(4):
                    pt, ptb = nps()
                    proj_fm(wslot, wbuf, h, lambda k: hT[hb][:, k, :], 8, [hB[hb]], pt, ptb)
                    P.op(ACT, lambda e, pt=pt: e.activation(out=g1[:], in_=pt[:], func=AF.Square), reads=[ptb], writes=[gB[0]])
                    p2, p2b = nps()
                    P.op(PE, lambda e, p2=p2: e.matmul(p2[:], lhsT=blk[:], rhs=g1[:], start=True, stop=True), reads=[gB[0], cB], writes=[p2b])
                    rsqrt_from(f1[:], p2[:], 1.0 / 64, [p2b], [fB[0]])
                    if which == 0:
                        dst, dB = qT[:, h, :], [qB]
                    else:
                        dst, dB = Kc[:, h, tok0:tok0 + TT], [KB[T]]
                    P.op(DVE, lambda e, pt=pt, dst=dst: e.scalar_tensor_tensor(out=dst, in0=pt[:], scalar=gcol[:, 0:1], in1=f1[:], op0=ALU.mult, op1=ALU.mult),
                         reads=[ptb, fB[0], sB], writes=dB)
            items.append((12.0, lambda: qk(0)))
            items.append((12.0, lambda: qk(1)))

            def vproj():
                wslot, wbuf = wload(w_cols(wb_in, 1024), (8, 512))
                for s in range(4):
                    pt, ptb = nps()
                    for k in range(8):
                        P.op(PE, lambda e, k=k, s=s, pt=pt: e.matmul(pt[:], lhsT=hT[hb][:, k, s * 128:(s + 1) * 128], rhs=wslot[:, k, :], start=(k == 0), stop=(k == 7)),
                             reads=[wbuf, hB[hb]], writes=[ptb], ms=(k == 7))
                    P.op(ACT, lambda e, pt=pt, s=s: e.activation(out=Vc[:, T * 4 + s, :], in_=pt[:], func=AF.Copy), reads=[ptb], writes=[VB[T]])
                if T == 0:
                    dump("qT", qT[:], [128, 4, TT], BF16, [qB])
                    dump("kT", Kc[:, :, 0:TT], [128, 4, TT], BF16, [KB[0]])
                    dump("V", Vc[:, 0:4, :], [128, 4, 512], BF16, [VB[0]])
            items.append((8.0, vproj))

            nkt = 4 * T + 4
            O = [bank(0), bank(1)]
            Z = [bank(2), bank(3)]
            Sb = [[bank(4), bank(5)], [bank(6), bank(7)]]

            def qrange(kt):
                r = kt - 4 * T
                q0 = max(0, 128 * r)
                return r, q0, TT - q0

            def s_mm(h, kt):
                r, q0, n = qrange(kt)
                for a in range(2):
                    sp, spb = Sb[kt % 2][a]
                    P.op(PE, lambda e, sp=sp, a=a: e.matmul(sp[:, 0:n], lhsT=Kc[64 * a:64 * a + 64, h, kt * 128:(kt + 1) * 128],
                                                            rhs=qT[64 * a:64 * a + 64, h, q0:TT], start=True, stop=True),
                         reads=[KB[kt // 4], qB], writes=[spb], ms=True)

            def att_iter(h, kt):
                r, q0, n = qrange(kt)
                par = kt % 2
                if kt == 0:
                    s_mm(h, 0)
                if kt + 1 < nkt:
                    s_mm(h, kt + 1)
                for a in range(2):
                    sp, spb = Sb[par][a]
                    P.op(ACT, lambda e, sp=sp, a=a: e.activation(out=e_t[par][a][:, 0:n], in_=sp[:, 0:n], func=AF.Exp, scale=0.125), reads=[spb], writes=[eB[par][a]])
                    if r >= 0:
                        P.op(POOL, lambda e, a=a: e.memset(e_t[par][a][64:128, 0:64], 0.0), reads=[], writes=[eB[par][a]])
                lastkt = (kt == nkt - 1)
                for a in range(2):
                    P.op(PE, lambda e, a=a: e.matmul(O[a][0][:, q0:TT], lhsT=Vc[:, kt, h * 128:(h + 1) * 128], rhs=e_t[par][a][:, 0:n], start=(kt == 0), stop=False),
                         reads=[VB[kt // 4], eB[par][a]], writes=[O[a][1]], ms=False)
                    P.op(PE, lambda e, a=a: e.matmul(Z[a][0][:, q0:TT], lhsT=ones[:], rhs=e_t[par][a][:, 0:n], start=(kt == 0), stop=False),
                         reads=[cB, eB[par][a]], writes=[Z[a][1]], ms=(not lastkt))
                    if lastkt:
                        P.op(PE, lambda e, a=a: e.matmul(O[a][0][:], lhsT=zeros[:], rhs=qT[:, h, :], start=False, stop=True),
                             reads=[cB, qB], writes=[O[a][1]], ms=False)
                        P.op(PE, lambda e, a=a: e.matmul(Z[a][0][:], lhsT=zeros[:], rhs=qT[:, h, :], start=False, stop=True),
                             reads=[cB, qB], writes=[Z[a][1]], ms=True)

            def att_fin_a(h):
                for a in range(2):
                    fa, fb = (f1, fB[0]) if a == 0 else (f2, fB[1])
                    P.op(ACT, lambda e, a=a, fa=fa: e.activation(out=fa[:], in_=Z[a][0][:], func=AF.Ln), reads=[Z[a][1]], writes=[fb])
                    P.op(ACT, lambda e, fa=fa: e.activation(out=fa[:], in_=fa[:], func=AF.Exp, scale=-1.0), reads=[fb], writes=[fb])
                    P.op(DVE, lambda e, a=a, fa=fa: e.tensor_tensor(out=fa[:], in0=O[a][0][:], in1=fa[:], op=ALU.mult), reads=[O[a][1], fb], writes=[fb])
                P.op(DVE, lambda e: e.scalar_tensor_tensor(out=f1[:], in0=f2[:], scalar=neglam[:, 0:1], in1=f1[:], op0=ALU.mult, op1=ALU.add),
                     reads=[fB[0], fB[1], sB], writes=[fB[0]])
                P.op(POOL, lambda e: e.tensor_tensor(out=g2[:], in0=f1[:], in1=f1[:], op=ALU.mult), reads=[fB[0]], writes=[gB[1]])

            def att_fin_b(h):
                sp, spb = Sb[0][0]
                P.op(PE, lambda e: e.matmul(sp[:], lhsT=ones[:], rhs=g2[:], start=True, stop=True), reads=[cB, gB[1]], writes=[spb])
                rsqrt_from(f2[:], sp[:], 1.0 / 128, [spb], [fB[1]])
                P.op(DVE, lambda e: e.tensor_tensor(out=aT_[:, h, :], in0=f1[:], in1=f2[:], op=ALU.mult), reads=[fB[0], fB[1]], writes=[aB])
                if T == 0 and h == 3:
                    dump("aT", aT_[:], [128, 4, TT], BF16, [aB])

            for h in range(4):
                for kt in range(nkt):
                    items.append((1.6, lambda h=h, kt=kt: att_iter(h, kt)))
                    if kt == 0 and h >= 1:
                        items.append((2.0, lambda h=h: att_fin_b(h - 1)))
                items.append((2.0, lambda h=h: att_fin_a(h)))
            items.append((2.0, lambda: att_fin_b(3)))
            n_ab = len(items)

            pj = {}

            def projs(i):
                if i == 0:
                    pj["a"] = wload(w_rows(wb_pa, 0), (4, 1024))
                    pj["s"] = wload(w_rows(wb_ps, 0), (4, 1024))
                wpa, wpaB = pj["a"]
                wps_, wpsB = pj["s"]
                pa, pab = nps()
                pss, pssb = nps()
                for k in range(4):
                    P.op(PE, lambda e, k=k: e.matmul(pa[:], lhsT=wpa[:, k, i * 128:(i + 1) * 128], rhs=aT_[:, k, :], start=(k == 0), stop=(k == 3)),
                         reads=[wpaB, aB], writes=[pab], ms=(k == 3))
                for k in range(4):
                    P.op(PE, lambda e, k=k: e.matmul(pss[:], lhsT=wps_[:, k, i * 128:(i + 1) * 128], rhs=sT[hb][:, k, :], start=(k == 0), stop=(k == 3)),
                         reads=[wpsB, stB_[hb]], writes=[pssb], ms=(k == 3))
                P.op(ACT, lambda e: e.activation(out=mrg[:, 8 + i, :], in_=pa[:], func=AF.Copy, scale=0.5), reads=[pab], writes=[mB])
                P.op(DVE, lambda e: e.tensor_scalar(out=mrg[:, i, :], in0=pss[:], scalar1=0.5, scalar2=None, op0=ALU.mult), reads=[pssb], writes=[mB])
            for i in range(8):
                items.append((2.0, lambda i=i: projs(i)))

            gst = {}

            def gates(gsel, half, ii):
                if ii == 0:
                    gst["w"] = wload(w_cols(wb_in, 2048 + 1024 * gsel + 512 * half), (8, 512))
                wslot, wbuf = gst["w"]
                i = half * 4 + ii
                pt, ptb = nps()
                proj_fm(wslot, wbuf, ii, lambda k: hT[hb][:, k, :], 8, [hB[hb]], pt, ptb)
                P.op(ACT, lambda e: e.activation(out=g1[:], in_=pt[:], func=AF.Tanh, scale=0.5, bias=bgate[:, 8 * gsel + i:8 * gsel + i + 1]),
                     reads=[ptb, sB], writes=[gB[0]])
                if gsel == 0:
                    P.op(DVE, lambda e: e.scalar_tensor_tensor(out=mrg[:, 8 + i, :], in0=g1[:], scalar=1.0, in1=mrg[:, 8 + i, :], op0=ALU.add, op1=ALU.mult), reads=[mB, gB[0]], writes=[mB])
                else:
                    P.op(DVE, lambda e: e.scalar_tensor_tensor(out=mrg[:, i, :], in0=g1[:], scalar=1.0, in1=mrg[:, i, :], op0=ALU.add, op1=ALU.mult), reads=[mB, gB[0]], writes=[mB])
                    P.op(POOL, lambda e: e.tensor_tensor(out=mrg[:, i, :], in0=mrg[:, i, :], in1=mrg[:, 8 + i, :], op=ALU.add), reads=[mB], writes=[mB])
                if T == 0 and gsel == 1 and half == 1 and ii == 3:
                    dump("mrg", mrg[:, 0:8, :], [128, 8, TT], BF16, [mB])
            for gsel in range(2):
                for half in range(2):
                    for ii in range(4):
                        items.append((2.0, lambda gsel=gsel, half=half, ii=ii: gates(gsel, half, ii)))

            wst = {}

            def wout(s, half):
                if s == 0 and half == 0:
                    wst["w"] = [wload(w_rows(wb_out, 512 * i), (4, 1024)) for i in range(2)]
                wo = wst["w"]
                pt, ptb = nps()
                for k in range(8):
                    ws_, wb_ = wo[k // 4]
                    P.op(PE, lambda e, k=k, ws_=ws_: e.matmul(pt[:], lhsT=mrg[:, k, s * 128:(s + 1) * 128], rhs=ws_[:, k % 4, half * 512:(half + 1) * 512],
                                                           start=(k == 0), stop=(k == 7)),
                         reads=[wb_, mB], writes=[ptb], ms=(k == 7))
                P.op(DVE, lambda e: e.tensor_tensor(out=xt[:, s, half * 512:(half + 1) * 512], in0=xt[:, s, half * 512:(half + 1) * 512], in1=pt[:], op=ALU.add),
                     reads=[ptb, xB[s]], writes=[xB[s]])
                if T == 0 and s == 3 and half == 1:
                    dump("x1", xt[:], [128, 4, D], F32, xB)
            for s in range(4):
                for half in range(2):
                    items.append((2.0, lambda s=s, half=half: wout(s, half)))

            def norm2(s):
                pt, ptb = nps()
                norm_sub(xt[:, s, :], xB[s], hT[hb], hB[hb], s, pt, ptb)
            for s in range(4):
                items.append((3.0, lambda s=s: norm2(s)))

            mst = {}

            def mlp_in(ffh, q4, ii):
                if ii == 0:
                    mst["i"] = wload(w_cols(wb_mi, ffh * 2048 + q4 * 512), (8, 512))
                wslot, wbuf = mst["i"]
                fc = q4 * 4 + ii
                pt, ptb = nps()
                proj_fm(wslot, wbuf, ii, lambda k: hT[hb][:, k, :], 8, [hB[hb]], pt, ptb)
                rb, rbB = e_t[(fc // 2) % 2][fc % 2], eB[(fc // 2) % 2][fc % 2]
                P.op(ACT, lambda e: e.activation(out=rb[:], in_=pt[:], func=AF.Relu), reads=[ptb], writes=[rbB])
                P.op(POOL, lambda e: e.tensor_tensor(out=mrg[:, fc, :], in0=rb[:], in1=rb[:], op=ALU.mult), reads=[rbB], writes=[mB])

            def mlp_out(ffh, ps_, q4, kk):
                if kk == 0:
                    mst["o"] = wload(w_rows(wb_mo, ffh * 2048 + q4 * 512), (4, 1024))
                wslot, wbuf = mst["o"]
                fc = q4 * 4 + kk
                for s2 in range(2):
                    for half in range(2):
                        pt, ptb = bank(s2 * 2 + half)
                        s = 2 * ps_ + s2
                        P.op(PE, lambda e, s=s, half=half, pt=pt: e.matmul(pt[:], lhsT=mrg[:, fc, s * 128:(s + 1) * 128],
                                                                         rhs=wslot[:, kk, half * 512:(half + 1) * 512], start=(fc == 0), stop=(fc == 15)),
                             reads=[wbuf, mB], writes=[ptb], ms=(fc == 15 or (kk == 3 and s2 == 1 and half == 1)))

            def mlp_evac(ps_):
                for s2 in range(2):
                    for half in range(2):
                        pt, ptb = bank(s2 * 2 + half)
                        s = 2 * ps_ + s2
                        P.op(DVE, lambda e, s=s, half=half, pt=pt: e.tensor_tensor(out=xt[:, s, half * 512:(half + 1) * 512], in0=xt[:, s, half * 512:(half + 1) * 512], in1=pt[:], op=ALU.add),
                             reads=[ptb, xB[s]], writes=[xB[s]])

            for ffh in range(2):
                for q4 in range(4):
                    for ii in range(4):
                        items.append((2.0, lambda ffh=ffh, q4=q4, ii=ii: mlp_in(ffh, q4, ii)))
                for ps_ in range(2):
                    for q4 in range(4):
                        for kk in range(4):
                            items.append((1.0, lambda ffh=ffh, ps_=ps_, q4=q4, kk=kk: mlp_out(ffh, ps_, q4, kk)))
                    items.append((1.0, lambda ps_=ps_: mlp_evac(ps_)))

            def store():
                for s in range(4):
                    ob = Buf()
                    out_toks.append(P.dma(out_d[tok0 + s * 128: tok0 + (s + 1) * 128, :], xt[:, s, :], reads=[xB[s]], writes=[ob]))
            items.append((0.5, store))
            return items[:n_ab], items[n_ab:]

        LEAD = 0.0

        def run_merged(mi, li):
            tm = sum(c for c, _ in mi) or 1.0
            tl = sum(c for c, _ in li) or 1.0
            cm = 0.0
            cl = 0.0
            j = 0
            for c, f in mi:
                f()
                cm += c
                while j < len(li) and cl / tl < cm / tm + LEAD:
                    li[j][1]()
                    cl += li[j][0]
                    j += 1
            while j < len(li):
                li[j][1]()
                j += 1

        INTERLEAVE = True
        run_merged(lookahead_items(0), late_prep)
        P.barrier()
        for T in range(NT):
            ab, cc = main_items(T)
            li = lookahead_items(T + 1) if T + 1 < NT else []
            run_merged(ab[:4], li[:5])
            li = li[5:]
            for c, f in ab[4:]:
                f()
            if INTERLEAVE:
                run_merged(cc, li)
            else:
                for c, f in cc:
                    f()
                for c, f in li:
                    f()

        P.wait_tok(SP, out_toks + list(dbg_outs.values()))
        print("instructions recorded:", P.n_instr)
        P.emit()
    return nc


def _consts():
    bf = ml_dtypes.bfloat16
    ident = np.eye(128, dtype=np.float32).astype(bf)
    ones = np.ones((128, 128), np.float32).astype(bf)
    blk = np.zeros((128, 128), np.float32)
    blk[:64, :64] = 1.0
    blk[64:, 64:] = 1.0
    blk = blk.astype(bf)
    iota = np.broadcast_to(np.arange(512, dtype=np.float32)[None, :], (128, 512)).copy()
    rmask = np.zeros((128, 8), np.float32)
    for r in range(128):
        rmask[r, r // 16] = 1.0
    hsel = np.zeros((128, 8), np.float32)
    hsel[:64, 0] = 1.0
    hsel[64:, 1] = 1.0
    hsel[:64, 2] = 1.0
    hsel[64:, 2] = -1.0
    hsel[:, 3] = -math.pi
    hsel[:, 4] = EPS
    hsel[:, 5] = 0.5 * math.pi
    hsel[:, 6] = 0.5
    hsel[:, 7] = 0.25
    return {"c_ident": ident, "c_ones": ones, "c_blk": blk, "c_iota": iota, "c_rmask": rmask, "c_hsel": hsel}


_W_KEYS = ["w_in", "w_mlp_in", "w_mlp_out", "w_out", "w_proj_attn", "w_proj_ssm", "w_glu", "norm_mix_g", "norm_mlp_g",
           "b_gate", "q_norm_g", "k_norm_g", "lambda_q1", "lambda_k1", "lambda_q2", "lambda_k2", "subln_g",
           "ssm_a_re", "ssm_a_im", "ssm_log_dt", "ssm_b_re", "ssm_b_im", "ssm_c_re", "ssm_c_im", "ssm_d", "b_glu"]


def make_in_maps(inputs, n_cores, S):
    shared = _consts()
    for k in _W_KEYS:
        shared[k] = np.ascontiguousarray(np.asarray(inputs[k], dtype=np.float32)[0])
    x = np.asarray(inputs["x"], dtype=np.float32)
    maps = []
    for c in range(n_cores):
        m = dict(shared)
        m["x"] = np.ascontiguousarray(x[c, :S])
        maps.append(m)
    return maps


def kernel(**inputs):
    x = np.asarray(inputs["x"])
    B, S, _ = x.shape
    nc = build(S=S)
    in_maps = make_in_maps(inputs, B, S)
    res = run_bass_kernel_spmd(nc, in_maps, core_ids=list(range(B)))
    out = np.stack([np.asarray(r["out"]) for r in res.results], axis=0)
    return out.astype(np.float32)
```
